# Optimizing a Trainium2 kernel written in Bass

```python
import jax, jax.numpy as jnp
from jax import lax
import numpy as np

D_MODEL = 1024
BATCH = 32
SEQ = 2048
DEPTH = 1

CHUNK = 64
MIX_WIDTH = D_MODEL
POOL_WIDTH = MIX_WIDTH // 2
POOL_WINDOWS = (2, 4, 8, 16)
N_POOL_GROUPS = len(POOL_WINDOWS)
POOL_GROUP = POOL_WIDTH // N_POOL_GROUPS
SB_WIDTH = MIX_WIDTH - POOL_WIDTH
SB_HEAD_DIM = 64
SB_HEADS = SB_WIDTH // SB_HEAD_DIM
Q_BLOCK = 128
IN_WIDTH = 2 * POOL_WIDTH + 4 * SB_WIDTH
EPS = 1e-6

kernel_name = "hybrid_pool_stickbreak_block"


def rmsnorm(x, g):
    x32 = x.astype(jnp.float32)
    y = x32 * lax.rsqrt(jnp.mean(x32 * x32, axis=-1, keepdims=True) + EPS)
    return y.astype(x.dtype) * g


def pool_mixer(u, w_pool, pool_scale):
    b, s, _ = u.shape
    u32 = u.astype(jnp.float32)
    pos = jnp.arange(s)
    outs = []
    for gi, w in enumerate(POOL_WINDOWS):
        ug = u32[..., gi * POOL_GROUP:(gi + 1) * POOL_GROUP]
        cs = jnp.cumsum(ug, axis=1)
        cs_shift = jnp.concatenate(
            [jnp.zeros((b, w, POOL_GROUP), jnp.float32), cs[:, :-w]], axis=1)
        count = jnp.minimum(pos + 1, w).astype(jnp.float32)[None, :, None]
        pooled = (cs - cs_shift) / count - ug
        outs.append(jnp.einsum('bsc,cd->bsd', pooled.astype(u.dtype), w_pool[gi]))
    return jnp.concatenate(outs, axis=-1) * pool_scale


def stick_breaking_attention(q, k, v):
    s_len = q.shape[2]
    inv_sqrt_d = 1.0 / np.sqrt(SB_HEAD_DIM)
    outs = []
    for i in range(s_len // Q_BLOCK):
        n_keys = (i + 1) * Q_BLOCK
        qb = q[:, :, i * Q_BLOCK:(i + 1) * Q_BLOCK]
        kb = k[:, :, :n_keys]
        vb = v[:, :, :n_keys]
        z = jnp.einsum('bhqd,bhkd->bhqk', qb, kb).astype(jnp.float32) * inv_sqrt_d
        qpos = i * Q_BLOCK + jnp.arange(Q_BLOCK)
        kpos = jnp.arange(n_keys)
        mask = kpos[None, :] < qpos[:, None]
        log_beta = jax.nn.log_sigmoid(z)
        log_1mb = jnp.where(mask, jax.nn.log_sigmoid(-z), 0.0)
        tail = lax.cumsum(log_1mb, axis=3, reverse=True) - log_1mb
        a = jnp.where(mask, jnp.exp(log_beta + tail), 0.0)
        outs.append(jnp.einsum('bhqk,bhkd->bhqd', a.astype(vb.dtype), vb))
    return jnp.concatenate(outs, axis=2)


def setup_inputs(seed: int = 0) -> dict:
    key = jax.random.key(seed)
    ks = jax.random.split(key, 10)
    f32 = jnp.float32
    x = jax.random.normal(ks[0], (BATCH, SEQ, D_MODEL), f32)
    c = jax.random.normal(ks[1], (BATCH, D_MODEL), f32)
    w_ada = jax.random.normal(ks[2], (D_MODEL, 3 * D_MODEL), f32) * (0.1 * D_MODEL ** -0.5)
    b_ada = jax.random.normal(ks[3], (3 * D_MODEL,), f32) * 0.01
    g_pre = 1.0 + 0.02 * jax.random.normal(ks[4], (D_MODEL,), f32)
    w_in = jax.random.normal(ks[5], (D_MODEL, IN_WIDTH), f32) * D_MODEL ** -0.5
    w_pool = jax.random.normal(ks[6], (N_POOL_GROUPS, POOL_GROUP, POOL_GROUP), f32) * POOL_GROUP ** -0.5
    pool_scale = 1.0 + 0.02 * jax.random.normal(ks[7], (POOL_WIDTH,), f32)
    w_out = jax.random.normal(ks[8], (MIX_WIDTH, D_MODEL), f32) * MIX_WIDTH ** -0.5
    g_post = 1.0 + 0.02 * jax.random.normal(ks[9], (D_MODEL,), f32)
    return {"x": x, "c": c, "w_ada": w_ada, "b_ada": b_ada, "g_pre": g_pre,
            "w_in": w_in, "w_pool": w_pool, "pool_scale": pool_scale,
            "w_out": w_out, "g_post": g_post}


def reference(x, c, w_ada, b_ada, g_pre, w_in, w_pool, pool_scale, w_out, g_post):
    b, s, _ = x.shape
    mod = jax.nn.silu(c) @ w_ada + b_ada
    shift, scale, gate = jnp.split(mod, 3, axis=-1)
    for _ in range(DEPTH):
        h = rmsnorm(x, g_pre) * (1.0 + scale[:, None, :]) + shift[:, None, :]
        p = h @ w_in
        u, g_pool, q, k, v, g_sb = jnp.split(
            p, np.cumsum([POOL_WIDTH, POOL_WIDTH, SB_WIDTH, SB_WIDTH, SB_WIDTH]), axis=-1)
        y_pool = pool_mixer(u, w_pool, pool_scale) * jax.nn.silu(g_pool)
        to_heads = lambda t: t.reshape(b, s, SB_HEADS, SB_HEAD_DIM).transpose(0, 2, 1, 3)
        o = stick_breaking_attention(to_heads(q), to_heads(k), to_heads(v))
        y_sb = o.transpose(0, 2, 1, 3).reshape(b, s, SB_WIDTH) * jax.nn.silu(g_sb)
        y = jnp.concatenate([y_pool, y_sb], axis=-1) @ w_out
        x = x + gate[:, None, :] * rmsnorm(y, g_post)
    return x
```

```python
import numpy as np
import ml_dtypes
from contextlib import ExitStack

import concourse.bass as bass
import concourse.mybir as mybir
from concourse.bass_utils import run_bass_kernel_spmd

F32 = mybir.dt.float32
BF16 = mybir.dt.bfloat16
AF = mybir.ActivationFunctionType
ALU = mybir.AluOpType

NCORES = 8
BPC = 4
S = 2048
D = 1024
EPS = 1e-6

ENGS = ("pe", "act", "dve", "pool", "sp")


class _Op:
    __slots__ = ("eng", "fn", "waits", "signal", "count", "dma", "dsem", "dval", "idx")

    def __init__(self, eng, fn, dma):
        self.eng = eng
        self.fn = fn
        self.waits = {}
        self.signal = False
        self.count = None
        self.dma = dma
        self.dsem = None
        self.dval = None


def I(method, *args, **kw):
    return (method, args, kw)


class Prog:
    def __init__(self):
        self.ops = {e: [] for e in ENGS}
        self.last_w = {}
        self.readers = {}
        self.dma_slots = {}

    def op(self, eng, fn, r=(), w=(), slot=None):
        o = _Op(eng, fn, slot is not None)
        if slot is not None:
            self.dma_slots[slot] = self.dma_slots.get(slot, 0) + 16
            o.dsem = slot
            o.dval = self.dma_slots[slot]
        deps = []
        for res in r:
            lw = self.last_w.get(res)
            if lw is not None:
                deps.append((lw, True))
        for res in w:
            lw = self.last_w.get(res)
            if lw is not None:
                deps.append((lw, True))
            for rd in self.readers.get(res, ()):
                deps.append((rd, False))
        ow = o.waits
        for d, raw in deps:
            if d.eng == eng and not d.dma:
                if (not raw) or eng == "pe":
                    continue
            if d.dma:
                key = ("d", d.dsem)
                if ow.get(key, 0) < d.dval:
                    ow[key] = d.dval
            else:
                d.signal = True
                prev = ow.get(d.eng)
                if prev is None or d.idx > prev.idx:
                    ow[d.eng] = d
        for res in r:
            self.readers.setdefault(res, []).append(o)
        for res in w:
            self.last_w[res] = o
            self.readers[res] = []
        o.idx = len(self.ops[eng])
        self.ops[eng].append(o)
        return o

    def emit(self, nc):
        for e in ENGS:
            c = 0
            for o in self.ops[e]:
                if o.signal and not o.dma:
                    c += 1
                    o.count = c
        with ExitStack() as es:
            sems = {e: es.enter_context(nc.semaphore("s_" + e)) for e in ENGS}
            dsems = {s: es.enter_context(nc.semaphore("d_%s" % (s,))) for s in self.dma_slots}
            block = es.enter_context(nc.Block())
            prog = self

            def run(engname, eng):
                waited = {}
                for o in prog.ops[engname]:
                    for key, val in o.waits.items():
                        if isinstance(key, tuple):
                            sem = dsems[key[1]]
                            v = val
                        else:
                            sem = sems[key]
                            v = val.count
                        if waited.get(key, 0) >= v:
                            continue
                        eng.wait_ge(sem, v)
                        waited[key] = v
                    fn = o.fn
                    ins = fn[1](eng) if fn[0] is None else getattr(eng, fn[0])(*fn[1], **fn[2])
                    if o.dma:
                        ins.then_inc(dsems[o.dsem], 16)
                    elif o.signal:
                        ins.then_inc(sems[engname], 1)
                if engname == "sp":
                    for s, v in prog.dma_slots.items():
                        eng.wait_ge(dsems[s], v)

            @block.tensor
            def _(eng):
                run("pe", eng)

            @block.scalar
            def _(eng):
                run("act", eng)

            @block.vector
            def _(eng):
                run("dve", eng)

            @block.gpsimd
            def _(eng):
                run("pool", eng)

            @block.sync
            def _(eng):
                run("sp", eng)


DEBUG = False
SCR_WORDS = 11456
BLK = 64


def build_nc():
    nc = bass.Bass("TRN2", target_bir_lowering=False)
    dt_in = lambda n, s: nc.dram_tensor(n, s, F32, kind="ExternalInput").ap()
    x_d = dt_in("x", [BPC * S, D])
    ct_d = dt_in("ct", [128, 32])
    cols_d = dt_in("cols", [128, 28])
    rows_d = dt_in("rows", [2, 1024])
    cst_d = dt_in("cst", [128, 320])
    wada_d = dt_in("w_ada", [D, 3 * D])
    win_d = dt_in("w_in", [D, 3 * D])
    wpool_d = dt_in("w_pool", [4, 128, 128])
    wout_d = dt_in("w_out", [D, D])
    out_d = nc.dram_tensor("out", [BPC * S, D], F32, kind="ExternalOutput").ap()

    dbg_d = {}
    if DEBUG:
        for n, shp, dt_ in (("d_gmod", [128, 32], F32), ("d_shiftc", [128, 32], F32), ("d_gateG", [128, 4096], F32),
                            ("d_hT", [128, 4096], BF16), ("d_qT", [128, 8192], BF16), ("d_kT", [128, 8192], BF16),
                            ("d_vv", [128, 8192], BF16), ("d_ycat", [128, 4096], BF16), ("d_uT", [128, 2112], F32),
                            ("d_sgp", [128, 2048], BF16), ("d_sgs", [128, 2048], BF16)):
            dbg_d[n] = nc.dram_tensor(n, shp, dt_, kind="ExternalOutput").ap()

    P = Prog()
    with ExitStack() as es:
        sbt = lambda n, s, d: es.enter_context(nc.sbuf_tensor("sb_" + n, s, d))
        w_in_bf = sbt("w_in_bf", [128, 8, 3072], BF16)
        w_out_bf = sbt("w_out_bf", [128, 8, 1024], BF16)
        w_pool_bf = sbt("w_pool_bf", [128, 4, 128], BF16)
        cst = sbt("cst", [128, 320], F32)
        idb = sbt("idb", [128, 128], BF16)
        mkb = sbt("mkb", [128, 128], BF16)
        cols = sbt("cols", [128, 28], F32)
        b1col = sbt("b1col", [128, 8], F32)
        neghalf = sbt("neghalf", [128, 4], F32)
        ctt = sbt("ctt", [128, 32], F32)
        sct = sbt("sct", [128, 32], F32)
        gmod = sbt("gmod", [128, 8, 4], F32)
        shiftc = sbt("shiftc", [128, 8, 4], F32)
        gateG = sbt("gateG", [128, 4, 1024], F32)
        qT = sbt("qT", [128, 4, 2048], BF16)
        kT = sbt("kT", [128, 4, 2048], BF16)
        vv = sbt("vv", [128, 16, 512], BF16)
        halo = sbt("halo", [128, 4, 16], F32)
        xt = [sbt("xt%d" % i, [128, 1024], F32) for i in range(4)]
        sgs = sbt("sgs", [128, 4, 512], BF16)
        ycat = sbt("ycat", [128, 8, 512], BF16)
        ss = sbt("ss", [128, 4], F32)
        rs = sbt("rs", [128, 4], F32)
        ss2 = sbt("ss2", [128, 4], F32)
        rs2 = sbt("rs2", [128, 4], F32)
        etmp = sbt("etmp", [128, 16], F32)
        scr = sbt("scr", [128, SCR_WORDS], F32)
        ps = es.enter_context(nc.psum_tensor("ps", [128, 4096], F32))

        idf = cst[:, 0:128]
        cnt = cst[:, 256:320]

        def K(a, b):
            return [("s", i) for i in range(a // BLK, (b - 1) // BLK + 1)]

        def sview(a, words, bf=False):
            ap = scr[:, a:a + words]
            if bf:
                ap = ap.bitcast(BF16)
            return ap

        def pbk(*banks):
            return [("pb", i) for i in banks]

        WST = [0, 2048]
        REP = 4096
        BGB = 8192
        GPB = 9216
        HT = 0
        UT = 2048
        SA = 4224
        SB = 4752
        PL = [5280, 5536]
        SIG = [5792, 6304]
        SGP = 6816
        JUNK = 7840
        NG, NPB, NA = 2, 3, 4
        LAG = 3
        GB = [i * 1024 for i in range(NG)]
        PB = [2048 + i * 1088 for i in range(NPB)]
        AB = [5312 + i * 512 for i in range(NA)]
        ATO = 7360
        YS = [2048, 3072]
        JUNK2 = 4096

        hT = sview(HT, 2048, True).rearrange("p (k t) -> p k t", k=8)
        uT = sview(UT, 2112).rearrange("p (g t) -> p g t", g=4)
        sA = sview(SA, 528)
        sB = sview(SB, 528)
        pl = [sview(a, 256, True) for a in PL]
        sig = [sview(a, 512) for a in SIG]
        sgp = sview(SGP, 1024, True).rearrange("p (g t) -> p g t", g=4)
        junk = sview(JUNK, 512, True)
        gbuf = [sview(a, 1024) for a in GB]
        pbuf = [sview(a, 1088) for a in PB]
        abuf = [sview(a, 512, True) for a in AB]
        AT = sview(ATO, 4096, True).rearrange("p (k t) -> p k t", k=16)
        ysb = [sview(a, 1024) for a in YS]
        junk2 = sview(JUNK2, 512, True)

        def hT_keys(kt):
            return K(HT + kt * 256, HT + kt * 256 + 256)

        def uT_keys(g):
            return K(UT + g * 528, UT + g * 528 + 528)

        def AT_keys(kb0, nb, j):
            return [("s", (ATO + kb * 256 + j * 64) // BLK) for kb in range(kb0, kb0 + nb)]

        def AT_keys_n(kb, N):
            return [("s", (ATO + kb * 256 + j * 64) // BLK) for j in range(N // 128)]

        cp_rr = [0]

        def copy_any(out, in_, r, w, scale=None, engs=("act", "dve")):
            e = engs[cp_rr[0] % len(engs)]
            cp_rr[0] += 1
            if e == "act":
                if scale is None:
                    P.op("act", I("activation", out=out, in_=in_, func=AF.Copy), r=r, w=w)
                else:
                    P.op("act", I("activation", out=out, in_=in_, func=AF.Copy, scale=scale), r=r, w=w)
            else:
                if scale is None:
                    P.op(e, I("tensor_copy", out=out, in_=in_), r=r, w=w)
                else:
                    P.op(e, I("tensor_scalar", out=out, in0=in_, scalar1=scale, scalar2=None, op0=ALU.mult), r=r, w=w)

        bank_rr = [0]

        def nextbank():
            b = bank_rr[0] % 7
            bank_rr[0] += 1
            return b

        def bank_ap(bk, n=512, off=0):
            return ps[:, bk * 512 + off: bk * 512 + off + n]

        P.op("sp", I("dma_start", out=cst[:], in_=cst_d), w=["cst"], slot="c0")
        P.op("sp", I("dma_start", out=cols[:], in_=cols_d), w=["cols"], slot="c1")
        P.op("sp", I("dma_start", out=ctt[:], in_=ct_d), w=["ctt"], slot="c2")
        bgb = sview(BGB, 1024)
        gpb = sview(GPB, 1024)
        P.op("sp", I("dma_start", out=bgb, in_=rows_d[0].partition_broadcast(128)), w=K(BGB, BGB + 1024), slot="c3")
        P.op("sp", I("dma_start", out=gpb, in_=rows_d[1].partition_broadcast(128)), w=K(GPB, GPB + 1024), slot="c4")
        P.op("dve", I("tensor_copy", out=idb[:], in_=cst[:, 0:128]), r=["cst"], w=["idb"])
        P.op("dve", I("tensor_copy", out=mkb[:], in_=cst[:, 128:256]), r=["cst"], w=["mkb"])
        P.op("pool", I("memset", neghalf[:], -0.5), w=["neghalf"])
        P.op("dve", I("memset", ps[:, 3584:3586], 1.0), w=[("pb", 7)])
        P.op("dve", I("tensor_scalar", out=b1col[:], in0=cols[:, 16:24], scalar1=1.0, scalar2=None, op0=ALU.add), r=["cols"], w=["b1col"])
        P.op("act", I("activation", out=sct[:], in_=ctt[:], func=AF.Sigmoid), r=["ctt"], w=["sct"])
        P.op("dve", I("tensor_tensor", out=sct[:], in0=sct[:], in1=ctt[:], op=ALU.mult), r=["sct", "ctt"], w=["sct"])
        rep = sview(REP, 4096).rearrange("p (k c) -> p k c", k=32)
        P.op("dve", I("tensor_copy", out=rep, in_=sct[:, 0:32].unsqueeze(2).to_broadcast([128, 32, 128])),
             r=["sct"], w=K(REP, REP + 4096))

        wst = [sview(a, 2048).rearrange("p (k f) -> p k f", k=8) for a in WST]
        wst_k = [K(a, a + 2048) for a in WST]
        stg = [0]

        def stage(dram2d, c0):
            i = stg[0] % 2
            stg[0] += 1
            src = dram2d.rearrange("(kt p) f -> p kt f", p=128)[:, :, c0:c0 + 256]
            P.op("sp", I("dma_start", out=wst[i], in_=src), w=wst_k[i], slot=("w", i))
            return i

        for pc in range(12):
            c0 = 256 * pc
            i = stage(wada_d, c0)
            if c0 < 2048:
                for half in range(2):
                    ft = (c0 + 128 * half) // 128
                    bk = nextbank()
                    for kt in range(8):
                        P.op("pe", I("matmul",
                            bank_ap(bk, 4), lhsT=wst[i][:, kt, 128 * half:128 * half + 128],
                            rhs=sct[:, 4 * kt:4 * kt + 4], start=(kt == 0), stop=(kt == 7)),
                            r=wst_k[i] + ["sct"], w=pbk(bk))
                    if ft < 8:
                        P.op("dve", I("tensor_scalar",
                            out=shiftc[:, ft, :], in0=bank_ap(bk, 4), scalar1=cols[:, 8 + ft:9 + ft], scalar2=None, op0=ALU.add),
                            r=pbk(bk) + ["cols"], w=["shiftc"])
                    else:
                        P.op("dve", I("tensor_scalar",
                            out=gmod[:, ft - 8, :], in0=bank_ap(bk, 4), scalar1=b1col[:, ft - 8:ft - 7],
                            scalar2=cols[:, ft - 8:ft - 7], op0=ALU.add, op1=ALU.mult),
                            r=pbk(bk) + ["cols", "b1col"], w=["gmod"])
            else:
                g0 = c0 - 2048
                for b in range(4):
                    bk = nextbank()
                    for kt in range(8):
                        P.op("pe", I("matmul",
                            bank_ap(bk, 256), lhsT=rep[:, 4 * kt + b, :], rhs=wst[i][:, kt, :],
                            start=(kt == 0), stop=(kt == 7)),
                            r=wst_k[i] + K(REP, REP + 4096), w=pbk(bk))
                    P.op("dve", I("tensor_tensor",
                        out=gateG[:, b, g0:g0 + 256], in0=bank_ap(bk, 256), in1=bgb[:, g0:g0 + 256], op=ALU.add),
                        r=pbk(bk) + K(BGB, BGB + 1024), w=[("gateG", b)])
                    P.op("pool", I("tensor_tensor",
                        out=gateG[:, b, g0:g0 + 256], in0=gateG[:, b, g0:g0 + 256], in1=gpb[:, g0:g0 + 256], op=ALU.mult),
                        r=[("gateG", b)] + K(GPB, GPB + 1024), w=[("gateG", b)])
        for pc in range(12):
            i = stage(win_d, 256 * pc)
            copy_any(w_in_bf[:, :, 256 * pc:256 * pc + 256], wst[i], r=wst_k[i], w=[("win", pc)], engs=("dve", "pool", "act"))
        for pc in range(4):
            i = stage(wout_d, 256 * pc)
            copy_any(w_out_bf[:, :, 256 * pc:256 * pc + 256], wst[i], r=wst_k[i], w=[("wout", pc)], engs=("dve", "pool", "act"))
        i = stg[0] % 2
        stg[0] += 1
        wpv = scr[:, WST[i]:WST[i] + 512].rearrange("p (g d) -> p g d", g=4)
        P.op("sp", I("dma_start", out=wpv, in_=wpool_d.rearrange("g c d -> c g d")), w=wst_k[i], slot=("w", i))
        P.op("dve", I("tensor_copy", out=w_pool_bf[:], in_=wpv), r=wst_k[i], w=["wpool"])

        xs_rr = [0]
        zs_rr = [0]
        ts_rr = [0]
        gs_rr = [0]
        ys_rr = [0]
        pl_rr = [0]
        sg_rr = [0]
        win_all = [("win", pc) for pc in range(12)]
        wout_all = [("wout", pc) for pc in range(4)]

        jobs = [(b, c) for b in range(BPC) for c in (3, 2, 1, 0)]
        NJ = len(jobs)

        def p1a_load(n, j):
            b, c = jobs[n]
            row0 = b * S + 512 * c + 128 * j
            xs = j % 2
            P.op("sp", I("dma_start", out=xt[xs][:], in_=x_d[row0:row0 + 128, :]), w=[("xt", xs)], slot=("x", xs))

        def p1a_elem(n, j):
            xs = j % 2
            P.op("act", I("activation", out=junk, in_=xt[xs][:], func=AF.Square, accum_out=ss[:, j:j + 1]),
                 r=[("xt", xs)], w=K(JUNK, JUNK + 512) + [("ss", j)])
            P.op("pool", I("tensor_scalar", out=rs[:, j:j + 1], in0=ss[:, j:j + 1], scalar1=1.0 / D, scalar2=EPS,
                           op0=ALU.mult, op1=ALU.add), r=[("ss", j)], w=[("rs", j)])
            P.op("pool", I("tensor_tensor", out=rs[:, j:j + 1], in0=rs[:, j:j + 1], in1=neghalf[:, 0:1], op=ALU.pow),
                 r=[("rs", j), "neghalf"], w=[("rs", j)])
            P.op("dve", I("tensor_scalar", out=xt[xs][:], in0=xt[xs][:], scalar1=rs[:, j:j + 1], scalar2=None,
                          op0=ALU.mult), r=[("xt", xs), ("rs", j)], w=[("xt", xs)])

        def p1a_tr(n, j):
            b, c = jobs[n]
            xs = j % 2
            p2, jj = j // 2, j % 2
            for k in range(8):
                bk = (k // 2 + 4 * p2) % 7
                P.op("pe", I("transpose", out=bank_ap(bk, 128, (k % 2) * 256 + 128 * jj), in_=xt[xs][:, 128 * k:128 * k + 128],
                             identity=idf), r=[("xt", xs), "cst"], w=pbk(bk))
            if jj == 1:
                for k in range(8):
                    bk = (k // 2 + 4 * p2) % 7
                    src = bank_ap(bk, 256, (k % 2) * 256)
                    dst = hT[:, k, 256 * p2:256 * p2 + 256]
                    if (k // 2) % 2 == 0:
                        P.op("act", I("activation", out=dst, in_=src, func=AF.Identity,
                                      scale=gmod[:, k, b:b + 1], bias=shiftc[:, k, b:b + 1]),
                             r=pbk(bk) + ["gmod", "shiftc"], w=hT_keys(k))
                    else:
                        P.op("dve", I("tensor_scalar", out=dst, in0=src, scalar1=gmod[:, k, b:b + 1],
                                      scalar2=shiftc[:, k, b:b + 1], op0=ALU.mult, op1=ALU.add),
                             r=pbk(bk) + ["gmod", "shiftc"], w=hT_keys(k))

        def phase1_rest(n):
                b, c = jobs[n]
                tau0 = 512 * c
                hT_all = K(HT, HT + 2048)
                if DEBUG and b == 0 and c == 3:
                    P.op("sp", I("dma_start", out=dbg_d["d_hT"], in_=sview(HT, 2048, True)), r=hT_all, slot="dbg")

                def proj_fm(ft):
                    bk = nextbank()
                    for kt in range(8):
                        P.op("pe", I("matmul", bank_ap(bk), lhsT=w_in_bf[:, kt, 128 * ft:128 * ft + 128],
                                                                     rhs=hT[:, kt, :], start=(kt == 0), stop=(kt == 7)),
                             r=[("win", ft // 2)] + hT_keys(kt), w=pbk(bk))
                    return bk

                for g in range(4):
                    bk = proj_fm(g)
                    copy_any(uT[:, g, 0:512], bank_ap(bk), r=pbk(bk), w=uT_keys(g))
                for g in range(4):
                    bk = proj_fm(4 + g)
                    sgi = sg_rr[0] % 2
                    sg_rr[0] += 1
                    P.op("act", I("activation", out=sig[sgi], in_=bank_ap(bk), func=AF.Sigmoid),
                         r=pbk(bk), w=K(SIG[sgi], SIG[sgi] + 512))
                    P.op("dve", I("tensor_tensor", out=sgp[:, g, :], in0=bank_ap(bk), in1=sig[sgi], op=ALU.mult),
                         r=pbk(bk) + K(SIG[sgi], SIG[sgi] + 512), w=K(SGP + 256 * g, SGP + 256 * g + 256))
                def proj_q():
                    for p in range(4):
                        bk = proj_fm(8 + p)
                        copy_any(qT[:, p, tau0:tau0 + 512], bank_ap(bk), r=pbk(bk), w=[("qT", p, c)], scale=0.125)

                def proj_k():
                    for p in range(4):
                        bk = proj_fm(12 + p)
                        copy_any(kT[:, p, tau0:tau0 + 512], bank_ap(bk), r=pbk(bk), w=[("kT", p, c)])

                def proj_gs():
                    for p in range(4):
                        bk = proj_fm(20 + p)
                        sgi = sg_rr[0] % 2
                        sg_rr[0] += 1
                        P.op("act", I("activation", out=sig[sgi], in_=bank_ap(bk), func=AF.Sigmoid),
                             r=pbk(bk), w=K(SIG[sgi], SIG[sgi] + 512))
                        P.op("dve", I("tensor_tensor", out=sgs[:, p, :], in0=bank_ap(bk), in1=sig[sgi], op=ALU.mult),
                             r=pbk(bk) + K(SIG[sgi], SIG[sgi] + 512), w=[("sgs", p)])

                def proj_v():
                    for j in range(4):
                        bk = nextbank()
                        blk = 4 * c + j
                        for kt in range(8):
                            P.op("pe", I("matmul", bank_ap(bk), lhsT=hT[:, kt, 128 * j:128 * j + 128],
                                         rhs=w_in_bf[:, kt, 2048:2560], start=(kt == 0), stop=(kt == 7)),
                                 r=[("win", 8), ("win", 9)] + hT_keys(kt), w=pbk(bk))
                        copy_any(vv[:, blk, :], bank_ap(bk), r=pbk(bk), w=[("v", blk)])

                def pool_elem(g):
                    w = 2 << g
                    if c == 3:
                        P.op("pool", I("memset", uT[:, g, 512:528], 0.0), w=K(UT + g * 528 + 512, UT + g * 528 + 528))
                    else:
                        P.op("pool", I("tensor_copy", out=uT[:, g, 512:528], in_=halo[:, g, :]),
                             r=[("halo", g)], w=K(UT + g * 528 + 512, UT + g * 528 + 528))
                    P.op("pool", I("tensor_tensor", out=sA[:, 0:527], in0=uT[:, g, 0:527], in1=uT[:, g, 1:528], op=ALU.add),
                         r=uT_keys(g), w=K(SA, SA + 528))
                    cur, curk, oth, othk = sA, K(SA, SA + 528), sB, K(SB, SB + 528)
                    ln = 527
                    step = 2
                    while step < w:
                        nl = ln - step
                        P.op("pool", I("tensor_tensor",
                            out=oth[:, 0:nl], in0=cur[:, 0:nl], in1=cur[:, step:step + nl], op=ALU.add), r=curk, w=othk)
                        cur, curk, oth, othk = oth, othk, cur, curk
                        ln = nl
                        step *= 2
                    pli = pl_rr[0] % 2
                    pl_rr[0] += 1
                    plk = K(PL[pli], PL[pli] + 256)
                    P.op("dve", I("scalar_tensor_tensor",
                        out=pl[pli], in0=cur[:, 0:512], scalar=1.0 / w, in1=uT[:, g, 0:512], op0=ALU.mult, op1=ALU.subtract),
                        r=curk + uT_keys(g), w=plk)
                    if c == 3:
                        lo = 513 - w
                        P.op("dve", I("tensor_tensor",
                            out=etmp[:, 0:w - 1], in0=cur[:, lo:512], in1=cnt[:, 16 * g:16 * g + w - 1], op=ALU.mult),
                            r=curk + ["cst"], w=["etmp"])
                        P.op("dve", I("tensor_tensor",
                            out=pl[pli][:, lo:512], in0=etmp[:, 0:w - 1], in1=uT[:, g, lo:512], op=ALU.subtract),
                            r=["etmp"] + uT_keys(g), w=plk)
                    P.op("pool", I("tensor_copy", out=halo[:, g, :], in_=uT[:, g, 0:16]), r=uT_keys(g), w=[("halo", g)])
                    return pli, plk

                def pool_mm(g, pli, plk):
                    bk = nextbank()
                    P.op("pe", I("matmul", bank_ap(bk), lhsT=w_pool_bf[:, g, :], rhs=pl[pli], start=True, stop=True),
                         r=["wpool"] + plk, w=pbk(bk))
                    P.op("dve", I("scalar_tensor_tensor",
                        out=ycat[:, g, :], in0=bank_ap(bk), scalar=cols[:, 24 + g:25 + g], in1=sgp[:, g, :], op0=ALU.mult, op1=ALU.mult),
                        r=pbk(bk) + ["cols"] + K(SGP + 256 * g, SGP + 256 * g + 256), w=[("yc", g)])

                pe0 = pool_elem(0)
                proj_q()
                pool_mm(0, *pe0)
                pe1 = pool_elem(1)
                proj_k()
                pool_mm(1, *pe1)
                pe2 = pool_elem(2)
                proj_gs()
                pool_mm(2, *pe2)
                pe3 = pool_elem(3)
                proj_v()
                pool_mm(3, *pe3)

                if DEBUG and b == 0 and c == 3:
                    P.op("sp", I("dma_start", out=dbg_d["d_uT"], in_=sview(UT, 2112)), r=K(UT, UT + 2112), slot="dbg")
                    P.op("sp", I("dma_start", out=dbg_d["d_sgp"], in_=sview(SGP, 1024, True)), r=K(SGP, SGP + 1024), slot="dbg")
                    P.op("sp", I("dma_start", out=dbg_d["d_sgs"], in_=sgs[:].rearrange("p g t -> p (g t)")), r=[("sgs", p) for p in range(4)], slot="dbg")
                if DEBUG and b == 0 and c == 0:
                    P.op("sp", I("dma_start", out=dbg_d["d_qT"], in_=qT[:].rearrange("p g t -> p (g t)")), r=[("qT", p, cc) for p in range(4) for cc in range(4)], slot="dbg")
                    P.op("sp", I("dma_start", out=dbg_d["d_kT"], in_=kT[:].rearrange("p g t -> p (g t)")), r=[("kT", p, cc) for p in range(4) for cc in range(4)], slot="dbg")
                    P.op("sp", I("dma_start", out=dbg_d["d_vv"], in_=vv[:].rearrange("p g t -> p (g t)")), r=[("v", kb) for kb in range(16)], slot="dbg")
                    P.op("sp", I("dma_start", out=dbg_d["d_gmod"], in_=gmod[:].rearrange("p g t -> p (g t)")), r=["gmod"], slot="dbg")
                    P.op("sp", I("dma_start", out=dbg_d["d_shiftc"], in_=shiftc[:].rearrange("p g t -> p (g t)")), r=["shiftc"], slot="dbg")
                    P.op("sp", I("dma_start", out=dbg_d["d_gateG"], in_=gateG[:].rearrange("p g t -> p (g t)")), r=[("gateG", bb) for bb in range(4)], slot="dbg")
        def attention(n):
                b, c = jobs[n]
                tau0 = 512 * c
                segs = []
                for p in range(4):
                    for e2 in range(2):
                        hs = []
                        for j in range(4):
                            tq = 128 * (4 * c + j)
                            k0 = tq
                            si = 0
                            while k0 < S:
                                wseg = min(1024, S - k0)
                                hs.append([p, e2, j, tq, k0, wseg, si, False])
                                k0 += wseg
                                si += 1
                        hs[-1][7] = True
                        segs.extend(hs)
                pend = []
                prev_ps = None
                prev_w = None
                hcount = [0]

                def stage2(item):
                    (p, e2, j, k0, wseg, gsl, last) = item
                    lo_p, hi_p = 64 * e2, 64 * e2 + 64
                    ts = ts_rr[0] % 2
                    ts_rr[0] += 1
                    tb = ps[:, 2048 + 512 * ts: 2048 + 512 * ts + 512].bitcast(BF16)
                    ak = K(AB[gsl], AB[gsl] + 512)
                    for kk in range(wseg // 128):
                        P.op("pe", I("transpose",
                            out=tb[:, 128 * kk:128 * kk + 128], in_=abuf[gsl][:, 128 * kk:128 * kk + 128], identity=idb[:]),
                            r=ak + ["idb"], w=pbk(4 + ts))
                    nb = wseg // 128
                    kb0 = k0 // 128
                    P.op("act", I("activation",
                        out=AT[:, kb0:kb0 + nb, 128 * j:128 * j + 128],
                        in_=tb[:, 0:wseg].rearrange("p (k t) -> p k t", t=128), func=AF.Copy),
                        r=pbk(4 + ts), w=AT_keys(kb0, nb, j))
                    if last:
                        zs_ = zs_rr[0] % 2
                        zs_rr[0] += 1
                        obk = 2 * zs_
                        ocol = 512 * obk
                        for kb in range(15, 4 * c - 1, -1):
                            i = kb - 4 * c
                            N = 512 if i >= 3 else 128 * (i + 1)
                            P.op("pe", I("matmul",
                                ps[:, ocol:ocol + N], lhsT=vv[:, kb, 128 * p:128 * p + 128], rhs=AT[:, kb, 0:N],
                                start=(kb == 15), stop=(kb == 4 * c)),
                                r=[("v", kb)] + AT_keys_n(kb, N), w=pbk(obk))
                        P.op("dve", I("tensor_tensor",
                            out=ycat[lo_p:hi_p, 4 + p, :], in0=ps[lo_p:hi_p, ocol:ocol + 512], in1=sgs[lo_p:hi_p, p, :], op=ALU.mult),
                            r=pbk(obk) + [("sgs", p)], w=[("yc", 4 + p, e2)])

                for (p, e2, j, tq, k0, wseg, si, last) in segs:
                    lo_p, hi_p = 64 * e2, 64 * e2 + 64
                    zs = zs_rr[0] % 2
                    zs_rr[0] += 1
                    gsl = gs_rr[0] % NG
                    psl = gs_rr[0] % NPB
                    asl = gs_rr[0] % NA
                    gs_rr[0] += 1
                    zcol = 1024 * zs
                    zk = pbk(2 * zs, 2 * zs + 1)
                    npc = (wseg + 511) // 512
                    for i in range(npc):
                        wi = min(512, wseg - 512 * i)
                        first = (si == 0 and i == 0)
                        kchunks = sorted(set([(k0 + 512 * i) // 512, (k0 + 512 * i + wi - 1) // 512]))
                        P.op("pe", I("matmul",
                            ps[:, zcol + 512 * i: zcol + 512 * i + wi], lhsT=qT[lo_p:hi_p, p, tq:tq + 128],
                            rhs=kT[lo_p:hi_p, p, k0 + 512 * i: k0 + 512 * i + wi], start=True, stop=(not first)),
                            r=[("qT", p, c)] + [("kT", p, kc) for kc in kchunks], w=pbk(2 * zs + i))
                        if first:
                            P.op("pe", I("matmul", ps[:, zcol:zcol + 128], lhsT=idb[:], rhs=mkb[:], start=False, stop=True),
                                 r=["idb", "mkb"], w=pbk(2 * zs))
                    gk = K(GB[gsl], GB[gsl] + 1024)
                    pk = K(PB[psl], PB[psl] + 1088)
                    ak = K(AB[asl], AB[asl] + 512)
                    P.op("act", I("activation",
                        out=gbuf[gsl][:, 0:wseg], in_=ps[:, zcol:zcol + wseg], func=AF.Sigmoid, scale=-1.0),
                        r=zk[:npc], w=gk)
                    if si == 0:
                        P.op("dve", I("tensor_tensor_scan",
                            out=pbuf[psl][:, 1:1 + wseg], data0=gbuf[gsl][:, 0:wseg], data1=ps[:, 3584:3585].to_broadcast([128, wseg]),
                            initial=1.0, op0=ALU.mult, op1=ALU.mult), r=gk + [("pb", 7)], w=pk)
                        P.op("pool", I("tensor_scalar", out=abuf[asl][:, 0:1], in0=pbuf[psl][:, 1:2], scalar1=-1.0, scalar2=1.0,
                                       op0=ALU.mult, op1=ALU.add), r=pk, w=ak)
                    else:
                        ppk = K(PB[prev_ps], PB[prev_ps] + 1088)
                        P.op("dve", I("tensor_tensor_scan",
                            out=pbuf[psl][:, 1:1 + wseg], data0=gbuf[gsl][:, 0:wseg], data1=ps[:, 3584:3585].to_broadcast([128, wseg]),
                            initial=pbuf[prev_ps][:, prev_w:prev_w + 1], op0=ALU.mult, op1=ALU.mult), r=gk + ppk + [("pb", 7)], w=pk)
                        P.op("pool", I("tensor_tensor", out=abuf[asl][:, 0:1], in0=pbuf[prev_ps][:, prev_w:prev_w + 1],
                                       in1=pbuf[psl][:, 1:2], op=ALU.subtract), r=pk + ppk, w=ak)
                    P.op("pool", I("tensor_tensor",
                        out=abuf[asl][:, 1:wseg], in0=pbuf[psl][:, 1:wseg], in1=pbuf[psl][:, 2:1 + wseg], op=ALU.subtract),
                        r=pk, w=ak)
                    prev_ps, prev_w = psl, wseg
                    pend.append((p, e2, j, k0, wseg, asl, last))
                    if len(pend) > LAG:
                        stage2(pend.pop(0))
                while pend:
                    stage2(pend.pop(0))

        def p3_reload(n, j):
            b, c = jobs[n]
            row0 = b * S + 512 * c + 128 * j
            xs = 2 + j % 2
            P.op("sp", I("dma_start", out=xt[xs][:], in_=x_d[row0:row0 + 128, :]), w=[("xt", xs)], slot=("x", xs))

        def phase3(n):
                b, c = jobs[n]
                tau0 = 512 * c
                yc_all = [("yc", g) for g in range(4)] + [("yc", 4 + p, e2) for p in range(4) for e2 in range(2)]
                if DEBUG and b == 0 and c == 3:
                    P.op("sp", I("dma_start", out=dbg_d["d_ycat"], in_=ycat[:].rearrange("p g t -> p (g t)")), r=yc_all, slot="dbg")
                for j in range(4):
                    row0 = b * S + tau0 + 128 * j
                    zs = j % 3
                    zcol = 1024 * zs
                    for half in range(2):
                        for ft in range(8):
                            P.op("pe", I("matmul",
                                ps[:, zcol + 512 * half: zcol + 512 * half + 512], lhsT=ycat[:, ft, 128 * j:128 * j + 128],
                                rhs=w_out_bf[:, ft, 512 * half:512 * half + 512], start=(ft == 0), stop=(ft == 7)),
                                r=yc_all + wout_all, w=pbk(2 * zs + half))
                    yk = pbk(2 * zs, 2 * zs + 1)
                    P.op("act", I("activation", out=junk2, in_=ps[:, zcol:zcol + 1024], func=AF.Square,
                                                                        accum_out=ss2[:, j:j + 1]),
                         r=yk, w=K(JUNK2, JUNK2 + 512) + [("ss2", j)])
                    P.op("pool", I("tensor_scalar", out=rs2[:, j:j + 1], in0=ss2[:, j:j + 1], scalar1=1.0 / D, scalar2=EPS,
                                                                 op0=ALU.mult, op1=ALU.add), r=[("ss2", j)], w=[("rs2", j)])
                    P.op("pool", I("tensor_tensor", out=rs2[:, j:j + 1], in0=rs2[:, j:j + 1], in1=neghalf[:, 0:1], op=ALU.pow),
                         r=[("rs2", j), "neghalf"], w=[("rs2", j)])
                    xs = 2 + j % 2
                    ysi = ys_rr[0] % 2
                    ys_rr[0] += 1
                    ysk = K(YS[ysi], YS[ysi] + 1024)
                    P.op("dve", I("scalar_tensor_tensor",
                        out=ysb[ysi], in0=ps[:, zcol:zcol + 1024], scalar=rs2[:, j:j + 1], in1=gateG[:, b, :],
                        op0=ALU.mult, op1=ALU.mult), r=yk + [("rs2", j), ("gateG", b)], w=ysk)
                    P.op("pool", I("tensor_tensor", out=ysb[ysi], in0=ysb[ysi], in1=xt[xs][:], op=ALU.add),
                         r=ysk + [("xt", xs)], w=ysk)
                    if j + 2 < 4:
                        p3_reload(n, j + 2)
                    P.op("sp", I("dma_start", out=out_d[row0:row0 + 128, :], in_=ysb[ysi]),
                         r=ysk, slot=("o", ysi))
        def phase1_second_half(n):
            p1a_elem(n, 2)
            p1a_elem(n, 3)
            p1a_tr(n, 2)
            p1a_tr(n, 3)
            phase1_rest(n)

        def phase1_tr01(n):
            p1a_tr(n, 0)
            p1a_tr(n, 1)
            p1a_load(n, 2)
            p1a_load(n, 3)

        p1a_load(0, 0)
        p1a_load(0, 1)
        p1a_elem(0, 0)
        p1a_elem(0, 1)
        phase1_tr01(0)
        phase1_second_half(0)
        for n in range(NJ):
            p3_reload(n, 0)
            p3_reload(n, 1)
            if n + 1 < NJ:
                p1a_load(n + 1, 0)
                p1a_load(n + 1, 1)
            attention(n)
            if n + 1 < NJ:
                p1a_elem(n + 1, 0)
                p1a_elem(n + 1, 1)
                phase1_tr01(n + 1)
            phase3(n)
            if n + 1 < NJ:
                phase1_second_half(n + 1)
        P.emit(nc)
    return nc


_NC_CACHE = {}


def kernel(x, c, w_ada, b_ada, g_pre, w_in, w_pool, pool_scale, w_out, g_post):
    f32 = np.float32
    x = np.asarray(x, f32)
    c = np.asarray(c, f32)
    w_ada = np.ascontiguousarray(np.asarray(w_ada, f32))
    b_ada = np.asarray(b_ada, f32)
    g_pre = np.asarray(g_pre, f32)
    w_in = np.ascontiguousarray(np.asarray(w_in, f32))
    w_pool = np.ascontiguousarray(np.asarray(w_pool, f32))
    pool_scale = np.asarray(pool_scale, f32)
    w_out = np.ascontiguousarray(np.asarray(w_out, f32))
    g_post = np.asarray(g_post, f32)

    ident = np.eye(128, dtype=f32)
    ii = np.arange(128)
    maskb = np.where(ii[None, :] <= ii[:, None], -30000.0, 0.0).astype(f32)
    cnt = np.ones((4, 16), f32)
    for g in range(4):
        w = 2 << g
        for i in range(w - 1):
            cnt[g, i] = 1.0 / (w - 1 - i)
    cst = np.concatenate([ident, maskb, np.broadcast_to(cnt.reshape(1, 64), (128, 64))], axis=1).astype(f32)
    cols = np.concatenate([g_pre.reshape(8, 128).T, b_ada[:2048].reshape(16, 128).T, pool_scale.reshape(4, 128).T], axis=1)
    cols = np.ascontiguousarray(cols, f32)
    rows = np.ascontiguousarray(np.stack([b_ada[2048:3072], g_post], axis=0), f32)

    in_maps = []
    for i in range(NCORES):
        xc = np.ascontiguousarray(x[BPC * i:BPC * (i + 1), ::-1, :]).reshape(BPC * S, D)
        cc = c[BPC * i:BPC * (i + 1)]
        ct = np.ascontiguousarray(cc.T.reshape(8, 128, BPC).transpose(1, 0, 2).reshape(128, 32), f32)
        in_maps.append({"x": xc, "ct": ct, "cols": cols, "rows": rows, "cst": cst, "w_ada": w_ada,
                        "w_in": w_in, "w_pool": w_pool, "w_out": w_out})
    if "nc" not in _NC_CACHE:
        _NC_CACHE["nc"] = build_nc()
    res = run_bass_kernel_spmd(_NC_CACHE["nc"], in_maps, core_ids=list(range(NCORES)))
    _NC_CACHE["res"] = res
    out = np.empty((NCORES * BPC, S, D), f32)
    for i in range(NCORES):
        o = np.asarray(res.results[i]["out"], f32).reshape(BPC, S, D)
        out[BPC * i:BPC * (i + 1)] = o[:, ::-1, :]
    return out
```

```python
import numpy as np
import ml_dtypes
from contextlib import ExitStack

import concourse.bass as bass
import concourse.mybir as mybir
from concourse.bass_utils import run_bass_kernel_spmd

F32 = mybir.dt.float32
BF16 = mybir.dt.bfloat16
AF = mybir.ActivationFunctionType
ALU = mybir.AluOpType

NCORES = 8
BPC = 4
S = 2048
D = 1024
EPS = 1e-6

ENGS = ("pe", "act", "dve", "pool", "sp")


class _Op:
    __slots__ = ("eng", "fn", "waits", "signal", "count", "dma", "dsem", "dval", "idx")

    def __init__(self, eng, fn, dma):
        self.eng = eng
        self.fn = fn
        self.waits = {}
        self.signal = False
        self.count = None
        self.dma = dma
        self.dsem = None
        self.dval = None


def I(method, *args, **kw):
    return (method, args, kw)


class Prog:
    def __init__(self):
        self.ops = {e: [] for e in ENGS}
        self.last_w = {}
        self.readers = {}
        self.dma_slots = {}

    def op(self, eng, fn, r=(), w=(), slot=None):
        o = _Op(eng, fn, slot is not None)
        if slot is not None:
            self.dma_slots[slot] = self.dma_slots.get(slot, 0) + 16
            o.dsem = slot
            o.dval = self.dma_slots[slot]
        deps = []
        for res in r:
            lw = self.last_w.get(res)
            if lw is not None:
                deps.append((lw, True))
        for res in w:
            lw = self.last_w.get(res)
            if lw is not None:
                deps.append((lw, True))
            for rd in self.readers.get(res, ()):
                deps.append((rd, False))
        ow = o.waits
        for d, raw in deps:
            if d.eng == eng and not d.dma:
                if (not raw) or eng == "pe":
                    continue
            if d.dma:
                key = ("d", d.dsem)
                if ow.get(key, 0) < d.dval:
                    ow[key] = d.dval
            else:
                d.signal = True
                prev = ow.get(d.eng)
                if prev is None or d.idx > prev.idx:
                    ow[d.eng] = d
        for res in r:
            self.readers.setdefault(res, []).append(o)
        for res in w:
            self.last_w[res] = o
            self.readers[res] = []
        o.idx = len(self.ops[eng])
        self.ops[eng].append(o)
        return o

    def emit(self, nc):
        for e in ENGS:
            c = 0
            for o in self.ops[e]:
                if o.signal and not o.dma:
                    c += 1
                    o.count = c
        with ExitStack() as es:
            sems = {e: es.enter_context(nc.semaphore("s_" + e)) for e in ENGS}
            dsems = {s: es.enter_context(nc.semaphore("d_%s" % (s,))) for s in self.dma_slots}
            block = es.enter_context(nc.Block())
            prog = self

            def run(engname, eng):
                waited = {}
                for o in prog.ops[engname]:
                    for key, val in o.waits.items():
                        if isinstance(key, tuple):
                            sem = dsems[key[1]]
                            v = val
                        else:
                            sem = sems[key]
                            v = val.count
                        if waited.get(key, 0) >= v:
                            continue
                        eng.wait_ge(sem, v)
                        waited[key] = v
                    fn = o.fn
                    ins = fn[1](eng) if fn[0] is None else getattr(eng, fn[0])(*fn[1], **fn[2])
                    if o.dma:
                        ins.then_inc(dsems[o.dsem], 16)
                    elif o.signal:
                        ins.then_inc(sems[engname], 1)
                if engname == "sp":
                    for s, v in prog.dma_slots.items():
                        eng.wait_ge(dsems[s], v)

            @block.tensor
            def _(eng):
                run("pe", eng)

            @block.scalar
            def _(eng):
                run("act", eng)

            @block.vector
            def _(eng):
                run("dve", eng)

            @block.gpsimd
            def _(eng):
                run("pool", eng)

            @block.sync
            def _(eng):
                run("sp", eng)


DEBUG = False
SCR_WORDS = 11456
BLK = 64


def build_nc():
    nc = bass.Bass("TRN2", target_bir_lowering=False)
    dt_in = lambda n, s: nc.dram_tensor(n, s, F32, kind="ExternalInput").ap()
    x_d = dt_in("x", [BPC * S, D])
    ct_d = dt_in("ct", [128, 32])
    cols_d = dt_in("cols", [128, 28])
    rows_d = dt_in("rows", [2, 1024])
    cst_d = dt_in("cst", [128, 320])
    wada_d = dt_in("w_ada", [D, 3 * D])
    win_d = dt_in("w_in", [D, 3 * D])
    wpool_d = dt_in("w_pool", [4, 128, 128])
    wout_d = dt_in("w_out", [D, D])
    out_d = nc.dram_tensor("out", [BPC * S, D], F32, kind="ExternalOutput").ap()

    dbg_d = {}
    if DEBUG:
        for n, shp, dt_ in (("d_gmod", [128, 32], F32), ("d_shiftc", [128, 32], F32), ("d_gateG", [128, 4096], F32),
                            ("d_hT", [128, 4096], BF16), ("d_qT", [128, 8192], BF16), ("d_kT", [128, 8192], BF16),
                            ("d_vv", [128, 8192], BF16), ("d_ycat", [128, 4096], BF16), ("d_uT", [128, 2112], F32),
                            ("d_sgp", [128, 2048], BF16), ("d_sgs", [128, 2048], BF16)):
            dbg_d[n] = nc.dram_tensor(n, shp, dt_, kind="ExternalOutput").ap()

    P = Prog()
    with ExitStack() as es:
        sbt = lambda n, s, d: es.enter_context(nc.sbuf_tensor("sb_" + n, s, d))
        w_in_bf = sbt("w_in_bf", [128, 8, 3072], BF16)
        w_out_bf = sbt("w_out_bf", [128, 8, 1024], BF16)
        w_pool_bf = sbt("w_pool_bf", [128, 4, 128], BF16)
        cst = sbt("cst", [128, 320], F32)
        idb = sbt("idb", [128, 128], BF16)
        mkb = sbt("mkb", [128, 128], BF16)
        cols = sbt("cols", [128, 28], F32)
        b1col = sbt("b1col", [128, 8], F32)
        neghalf = sbt("neghalf", [128, 4], F32)
        ctt = sbt("ctt", [128, 32], F32)
        sct = sbt("sct", [128, 32], F32)
        gmod = sbt("gmod", [128, 8, 4], F32)
        shiftc = sbt("shiftc", [128, 8, 4], F32)
        gateG = sbt("gateG", [128, 4, 1024], F32)
        qT = sbt("qT", [128, 4, 2048], BF16)
        kT = sbt("kT", [128, 4, 2048], BF16)
        vv = sbt("vv", [128, 16, 512], BF16)
        halo = sbt("halo", [128, 4, 16], F32)
        xt = [sbt("xt%d" % i, [128, 1024], F32) for i in range(4)]
        sgs = sbt("sgs", [128, 4, 512], BF16)
        ycat = sbt("ycat", [128, 8, 512], BF16)
        ss = sbt("ss", [128, 4], F32)
        rs = sbt("rs", [128, 4], F32)
        ss2 = sbt("ss2", [128, 4], F32)
        rs2 = sbt("rs2", [128, 4], F32)
        etmp = sbt("etmp", [128, 16], F32)
        scr = sbt("scr", [128, SCR_WORDS], F32)
        ps = es.enter_context(nc.psum_tensor("ps", [128, 4096], F32))

        idf = cst[:, 0:128]
        cnt = cst[:, 256:320]

        def K(a, b):
            return [("s", i) for i in range(a // BLK, (b - 1) // BLK + 1)]

        def sview(a, words, bf=False):
            ap = scr[:, a:a + words]
            if bf:
                ap = ap.bitcast(BF16)
            return ap

        def pbk(*banks):
            return [("pb", i) for i in banks]

        WST = [0, 2048]
        REP = 4096
        BGB = 8192
        GPB = 9216
        HT = 0
        UT = 2048
        SA = 4224
        SB = 4752
        PL = [5280, 5536]
        SIG = [5792, 6304]
        SGP = 6816
        JUNK = 7840
        NG, NPB, NA = 2, 3, 4
        LAG = 3
        GB = [i * 1024 for i in range(NG)]
        PB = [2048 + i * 1088 for i in range(NPB)]
        AB = [5312 + i * 512 for i in range(NA)]
        ATO = 7360
        YS = [2048, 3072]
        JUNK2 = 4096

        hT = sview(HT, 2048, True).rearrange("p (k t) -> p k t", k=8)
        uT = sview(UT, 2112).rearrange("p (g t) -> p g t", g=4)
        sA = sview(SA, 528)
        sB = sview(SB, 528)
        pl = [sview(a, 256, True) for a in PL]
        sig = [sview(a, 512) for a in SIG]
        sgp = sview(SGP, 1024, True).rearrange("p (g t) -> p g t", g=4)
        junk = sview(JUNK, 512, True)
        gbuf = [sview(a, 1024) for a in GB]
        pbuf = [sview(a, 1088) for a in PB]
        abuf = [sview(a, 512, True) for a in AB]
        AT = sview(ATO, 4096, True).rearrange("p (k t) -> p k t", k=16)
        ysb = [sview(a, 1024) for a in YS]
        junk2 = sview(JUNK2, 512, True)

        def hT_keys(kt):
            return K(HT + kt * 256, HT + kt * 256 + 256)

        def uT_keys(g):
            return K(UT + g * 528, UT + g * 528 + 528)

        def AT_keys(kb0, nb, j):
            return [("s", (ATO + kb * 256 + j * 64) // BLK) for kb in range(kb0, kb0 + nb)]

        def AT_keys_n(kb, N):
            return [("s", (ATO + kb * 256 + j * 64) // BLK) for j in range(N // 128)]

        cp_rr = [0]

        def copy_any(out, in_, r, w, scale=None, engs=("act", "dve")):
            e = engs[cp_rr[0] % len(engs)]
            cp_rr[0] += 1
            if e == "act":
                if scale is None:
                    P.op("act", I("activation", out=out, in_=in_, func=AF.Copy), r=r, w=w)
                else:
                    P.op("act", I("activation", out=out, in_=in_, func=AF.Copy, scale=scale), r=r, w=w)
            else:
                if scale is None:
                    P.op(e, I("tensor_copy", out=out, in_=in_), r=r, w=w)
                else:
                    P.op(e, I("tensor_scalar", out=out, in0=in_, scalar1=scale, scalar2=None, op0=ALU.mult), r=r, w=w)

        bank_rr = [0]

        def nextbank():
            b = bank_rr[0] % 7
            bank_rr[0] += 1
            return b

        def bank_ap(bk, n=512, off=0):
            return ps[:, bk * 512 + off: bk * 512 + off + n]

        P.op("sp", I("dma_start", out=cst[:], in_=cst_d), w=["cst"], slot="c0")
        P.op("sp", I("dma_start", out=cols[:], in_=cols_d), w=["cols"], slot="c1")
        P.op("sp", I("dma_start", out=ctt[:], in_=ct_d), w=["ctt"], slot="c2")
        bgb = sview(BGB, 1024)
        gpb = sview(GPB, 1024)
        P.op("sp", I("dma_start", out=bgb, in_=rows_d[0].partition_broadcast(128)), w=K(BGB, BGB + 1024), slot="c3")
        P.op("sp", I("dma_start", out=gpb, in_=rows_d[1].partition_broadcast(128)), w=K(GPB, GPB + 1024), slot="c4")
        P.op("dve", I("tensor_copy", out=idb[:], in_=cst[:, 0:128]), r=["cst"], w=["idb"])
        P.op("dve", I("tensor_copy", out=mkb[:], in_=cst[:, 128:256]), r=["cst"], w=["mkb"])
        P.op("pool", I("memset", neghalf[:], -0.5), w=["neghalf"])
        P.op("dve", I("memset", ps[:, 3584:3586], 1.0), w=[("pb", 7)])
        P.op("dve", I("tensor_scalar", out=b1col[:], in0=cols[:, 16:24], scalar1=1.0, scalar2=None, op0=ALU.add), r=["cols"], w=["b1col"])
        P.op("act", I("activation", out=sct[:], in_=ctt[:], func=AF.Sigmoid), r=["ctt"], w=["sct"])
        P.op("dve", I("tensor_tensor", out=sct[:], in0=sct[:], in1=ctt[:], op=ALU.mult), r=["sct", "ctt"], w=["sct"])
        rep = sview(REP, 4096).rearrange("p (k c) -> p k c", k=32)
        P.op("dve", I("tensor_copy", out=rep, in_=sct[:, 0:32].unsqueeze(2).to_broadcast([128, 32, 128])),
             r=["sct"], w=K(REP, REP + 4096))

        wst = [sview(a, 2048).rearrange("p (k f) -> p k f", k=8) for a in WST]
        wst_k = [K(a, a + 2048) for a in WST]
        stg = [0]

        def stage(dram2d, c0):
            i = stg[0] % 2
            stg[0] += 1
            src = dram2d.rearrange("(kt p) f -> p kt f", p=128)[:, :, c0:c0 + 256]
            P.op("sp", I("dma_start", out=wst[i], in_=src), w=wst_k[i], slot=("w", i))
            return i

        for pc in range(12):
            c0 = 256 * pc
            i = stage(wada_d, c0)
            if c0 < 2048:
                for half in range(2):
                    ft = (c0 + 128 * half) // 128
                    bk = nextbank()
                    for kt in range(8):
                        P.op("pe", I("matmul",
                            bank_ap(bk, 4), lhsT=wst[i][:, kt, 128 * half:128 * half + 128],
                            rhs=sct[:, 4 * kt:4 * kt + 4], start=(kt == 0), stop=(kt == 7)),
                            r=wst_k[i] + ["sct"], w=pbk(bk))
                    if ft < 8:
                        P.op("dve", I("tensor_scalar",
                            out=shiftc[:, ft, :], in0=bank_ap(bk, 4), scalar1=cols[:, 8 + ft:9 + ft], scalar2=None, op0=ALU.add),
                            r=pbk(bk) + ["cols"], w=["shiftc"])
                    else:
                        P.op("dve", I("tensor_scalar",
                            out=gmod[:, ft - 8, :], in0=bank_ap(bk, 4), scalar1=b1col[:, ft - 8:ft - 7],
                            scalar2=cols[:, ft - 8:ft - 7], op0=ALU.add, op1=ALU.mult),
                            r=pbk(bk) + ["cols", "b1col"], w=["gmod"])
            else:
                g0 = c0 - 2048
                for b in range(4):
                    bk = nextbank()
                    for kt in range(8):
                        P.op("pe", I("matmul",
                            bank_ap(bk, 256), lhsT=rep[:, 4 * kt + b, :], rhs=wst[i][:, kt, :],
                            start=(kt == 0), stop=(kt == 7)),
                            r=wst_k[i] + K(REP, REP + 4096), w=pbk(bk))
                    P.op("dve", I("tensor_tensor",
                        out=gateG[:, b, g0:g0 + 256], in0=bank_ap(bk, 256), in1=bgb[:, g0:g0 + 256], op=ALU.add),
                        r=pbk(bk) + K(BGB, BGB + 1024), w=[("gateG", b)])
                    P.op("pool", I("tensor_tensor",
                        out=gateG[:, b, g0:g0 + 256], in0=gateG[:, b, g0:g0 + 256], in1=gpb[:, g0:g0 + 256], op=ALU.mult),
                        r=[("gateG", b)] + K(GPB, GPB + 1024), w=[("gateG", b)])
        for pc in range(12):
            i = stage(win_d, 256 * pc)
            copy_any(w_in_bf[:, :, 256 * pc:256 * pc + 256], wst[i], r=wst_k[i], w=[("win", pc)], engs=("dve", "pool", "act"))
        for pc in range(4):
            i = stage(wout_d, 256 * pc)
            copy_any(w_out_bf[:, :, 256 * pc:256 * pc + 256], wst[i], r=wst_k[i], w=[("wout", pc)], engs=("dve", "pool", "act"))
        i = stg[0] % 2
        stg[0] += 1
        wpv = scr[:, WST[i]:WST[i] + 512].rearrange("p (g d) -> p g d", g=4)
        P.op("sp", I("dma_start", out=wpv, in_=wpool_d.rearrange("g c d -> c g d")), w=wst_k[i], slot=("w", i))
        P.op("dve", I("tensor_copy", out=w_pool_bf[:], in_=wpv), r=wst_k[i], w=["wpool"])

        xs_rr = [0]
        zs_rr = [0]
        ts_rr = [0]
        gs_rr = [0]
        ys_rr = [0]
        pl_rr = [0]
        sg_rr = [0]
        win_all = [("win", pc) for pc in range(12)]
        wout_all = [("wout", pc) for pc in range(4)]

        jobs = [(b, c) for b in range(BPC) for c in (3, 2, 1, 0)]
        NJ = len(jobs)

        def p1a_load(n, j):
            b, c = jobs[n]
            row0 = b * S + 512 * c + 128 * j
            xs = j % 2
            P.op("sp", I("dma_start", out=xt[xs][:], in_=x_d[row0:row0 + 128, :]), w=[("xt", xs)], slot=("x", xs))

        def p1a_elem(n, j):
            xs = j % 2
            P.op("act", I("activation", out=junk, in_=xt[xs][:], func=AF.Square, accum_out=ss[:, j:j + 1]),
                 r=[("xt", xs)], w=K(JUNK, JUNK + 512) + [("ss", j)])
            P.op("pool", I("tensor_scalar", out=rs[:, j:j + 1], in0=ss[:, j:j + 1], scalar1=1.0 / D, scalar2=EPS,
                           op0=ALU.mult, op1=ALU.add), r=[("ss", j)], w=[("rs", j)])
            P.op("pool", I("tensor_tensor", out=rs[:, j:j + 1], in0=rs[:, j:j + 1], in1=neghalf[:, 0:1], op=ALU.pow),
                 r=[("rs", j), "neghalf"], w=[("rs", j)])
            P.op("dve", I("tensor_scalar", out=xt[xs][:], in0=xt[xs][:], scalar1=rs[:, j:j + 1], scalar2=None,
                          op0=ALU.mult), r=[("xt", xs), ("rs", j)], w=[("xt", xs)])

        def p1a_tr(n, j):
            b, c = jobs[n]
            xs = j % 2
            p2, jj = j // 2, j % 2
            for k in range(8):
                bk = (k // 2 + 4 * p2) % 7
                P.op("pe", I("transpose", out=bank_ap(bk, 128, (k % 2) * 256 + 128 * jj), in_=xt[xs][:, 128 * k:128 * k + 128],
                             identity=idf), r=[("xt", xs), "cst"], w=pbk(bk))
            if jj == 1:
                for k in range(8):
                    bk = (k // 2 + 4 * p2) % 7
                    src = bank_ap(bk, 256, (k % 2) * 256)
                    dst = hT[:, k, 256 * p2:256 * p2 + 256]
                    if (k // 2) % 2 == 0:
                        P.op("act", I("activation", out=dst, in_=src, func=AF.Identity,
                                      scale=gmod[:, k, b:b + 1], bias=shiftc[:, k, b:b + 1]),
                             r=pbk(bk) + ["gmod", "shiftc"], w=hT_keys(k))
                    else:
                        P.op("dve", I("tensor_scalar", out=dst, in0=src, scalar1=gmod[:, k, b:b + 1],
                                      scalar2=shiftc[:, k, b:b + 1], op0=ALU.mult, op1=ALU.add),
                             r=pbk(bk) + ["gmod", "shiftc"], w=hT_keys(k))

        def phase1_rest(n):
                b, c = jobs[n]
                tau0 = 512 * c
                hT_all = K(HT, HT + 2048)
                if DEBUG and b == 0 and c == 3:
                    P.op("sp", I("dma_start", out=dbg_d["d_hT"], in_=sview(HT, 2048, True)), r=hT_all, slot="dbg")

                def proj_fm(ft):
                    bk = nextbank()
                    for kt in range(8):
                        P.op("pe", I("matmul", bank_ap(bk), lhsT=w_in_bf[:, kt, 128 * ft:128 * ft + 128],
                                                                     rhs=hT[:, kt, :], start=(kt == 0), stop=(kt == 7)),
                             r=[("win", ft // 2)] + hT_keys(kt), w=pbk(bk))
                    return bk

                for g in range(4):
                    bk = proj_fm(g)
                    copy_any(uT[:, g, 0:512], bank_ap(bk), r=pbk(bk), w=uT_keys(g))
                for g in range(4):
                    bk = proj_fm(4 + g)
                    sgi = sg_rr[0] % 2
                    sg_rr[0] += 1
                    P.op("act", I("activation", out=sig[sgi], in_=bank_ap(bk), func=AF.Sigmoid),
                         r=pbk(bk), w=K(SIG[sgi], SIG[sgi] + 512))
                    P.op("dve", I("tensor_tensor", out=sgp[:, g, :], in0=bank_ap(bk), in1=sig[sgi], op=ALU.mult),
                         r=pbk(bk) + K(SIG[sgi], SIG[sgi] + 512), w=K(SGP + 256 * g, SGP + 256 * g + 256))
                def proj_q():
                    for p in range(4):
                        bk = proj_fm(8 + p)
                        copy_any(qT[:, p, tau0:tau0 + 512], bank_ap(bk), r=pbk(bk), w=[("qT", p, c)], scale=0.125)

                def proj_k():
                    for p in range(4):
                        bk = proj_fm(12 + p)
                        copy_any(kT[:, p, tau0:tau0 + 512], bank_ap(bk), r=pbk(bk), w=[("kT", p, c)])

                def proj_gs():
                    for p in range(4):
                        bk = proj_fm(20 + p)
                        sgi = sg_rr[0] % 2
                        sg_rr[0] += 1
                        P.op("act", I("activation", out=sig[sgi], in_=bank_ap(bk), func=AF.Sigmoid),
                             r=pbk(bk), w=K(SIG[sgi], SIG[sgi] + 512))
                        P.op("dve", I("tensor_tensor", out=sgs[:, p, :], in0=bank_ap(bk), in1=sig[sgi], op=ALU.mult),
                             r=pbk(bk) + K(SIG[sgi], SIG[sgi] + 512), w=[("sgs", p)])

                def proj_v():
                    for j in range(4):
                        bk = nextbank()
                        blk = 4 * c + j
                        for kt in range(8):
                            P.op("pe", I("matmul", bank_ap(bk), lhsT=hT[:, kt, 128 * j:128 * j + 128],
                                         rhs=w_in_bf[:, kt, 2048:2560], start=(kt == 0), stop=(kt == 7)),
                                 r=[("win", 8), ("win", 9)] + hT_keys(kt), w=pbk(bk))
                        copy_any(vv[:, blk, :], bank_ap(bk), r=pbk(bk), w=[("v", blk)])

                def pool_elem(g):
                    w = 2 << g
                    if c == 3:
                        P.op("pool", I("memset", uT[:, g, 512:528], 0.0), w=K(UT + g * 528 + 512, UT + g * 528 + 528))
                    else:
                        P.op("pool", I("tensor_copy", out=uT[:, g, 512:528], in_=halo[:, g, :]),
                             r=[("halo", g)], w=K(UT + g * 528 + 512, UT + g * 528 + 528))
                    P.op("pool", I("tensor_tensor", out=sA[:, 0:527], in0=uT[:, g, 0:527], in1=uT[:, g, 1:528], op=ALU.add),
                         r=uT_keys(g), w=K(SA, SA + 528))
                    cur, curk, oth, othk = sA, K(SA, SA + 528), sB, K(SB, SB + 528)
                    ln = 527
                    step = 2
                    while step < w:
                        nl = ln - step
                        P.op("pool", I("tensor_tensor",
                            out=oth[:, 0:nl], in0=cur[:, 0:nl], in1=cur[:, step:step + nl], op=ALU.add), r=curk, w=othk)
                        cur, curk, oth, othk = oth, othk, cur, curk
                        ln = nl
                        step *= 2
                    pli = pl_rr[0] % 2
                    pl_rr[0] += 1
                    plk = K(PL[pli], PL[pli] + 256)
                    P.op("dve", I("scalar_tensor_tensor",
                        out=pl[pli], in0=cur[:, 0:512], scalar=1.0 / w, in1=uT[:, g, 0:512], op0=ALU.mult, op1=ALU.subtract),
                        r=curk + uT_keys(g), w=plk)
                    if c == 3:
                        lo = 513 - w
                        P.op("dve", I("tensor_tensor",
                            out=etmp[:, 0:w - 1], in0=cur[:, lo:512], in1=cnt[:, 16 * g:16 * g + w - 1], op=ALU.mult),
                            r=curk + ["cst"], w=["etmp"])
                        P.op("dve", I("tensor_tensor",
                            out=pl[pli][:, lo:512], in0=etmp[:, 0:w - 1], in1=uT[:, g, lo:512], op=ALU.subtract),
                            r=["etmp"] + uT_keys(g), w=plk)
                    P.op("pool", I("tensor_copy", out=halo[:, g, :], in_=uT[:, g, 0:16]), r=uT_keys(g), w=[("halo", g)])
                    return pli, plk

                def pool_mm(g, pli, plk):
                    bk = nextbank()
                    P.op("pe", I("matmul", bank_ap(bk), lhsT=w_pool_bf[:, g, :], rhs=pl[pli], start=True, stop=True),
                         r=["wpool"] + plk, w=pbk(bk))
                    P.op("dve", I("scalar_tensor_tensor",
                        out=ycat[:, g, :], in0=bank_ap(bk), scalar=cols[:, 24 + g:25 + g], in1=sgp[:, g, :], op0=ALU.mult, op1=ALU.mult),
                        r=pbk(bk) + ["cols"] + K(SGP + 256 * g, SGP + 256 * g + 256), w=[("yc", g)])

                pe0 = pool_elem(0)
                proj_q()
                pool_mm(0, *pe0)
                pe1 = pool_elem(1)
                proj_k()
                pool_mm(1, *pe1)
                pe2 = pool_elem(2)
                proj_gs()
                pool_mm(2, *pe2)
                pe3 = pool_elem(3)
                proj_v()
                pool_mm(3, *pe3)

                if DEBUG and b == 0 and c == 3:
                    P.op("sp", I("dma_start", out=dbg_d["d_uT"], in_=sview(UT, 2112)), r=K(UT, UT + 2112), slot="dbg")
                    P.op("sp", I("dma_start", out=dbg_d["d_sgp"], in_=sview(SGP, 1024, True)), r=K(SGP, SGP + 1024), slot="dbg")
                    P.op("sp", I("dma_start", out=dbg_d["d_sgs"], in_=sgs[:].rearrange("p g t -> p (g t)")), r=[("sgs", p) for p in range(4)], slot="dbg")
                if DEBUG and b == 0 and c == 0:
                    P.op("sp", I("dma_start", out=dbg_d["d_qT"], in_=qT[:].rearrange("p g t -> p (g t)")), r=[("qT", p, cc) for p in range(4) for cc in range(4)], slot="dbg")
                    P.op("sp", I("dma_start", out=dbg_d["d_kT"], in_=kT[:].rearrange("p g t -> p (g t)")), r=[("kT", p, cc) for p in range(4) for cc in range(4)], slot="dbg")
                    P.op("sp", I("dma_start", out=dbg_d["d_vv"], in_=vv[:].rearrange("p g t -> p (g t)")), r=[("v", kb) for kb in range(16)], slot="dbg")
                    P.op("sp", I("dma_start", out=dbg_d["d_gmod"], in_=gmod[:].rearrange("p g t -> p (g t)")), r=["gmod"], slot="dbg")
                    P.op("sp", I("dma_start", out=dbg_d["d_shiftc"], in_=shiftc[:].rearrange("p g t -> p (g t)")), r=["shiftc"], slot="dbg")
                    P.op("sp", I("dma_start", out=dbg_d["d_gateG"], in_=gateG[:].rearrange("p g t -> p (g t)")), r=[("gateG", bb) for bb in range(4)], slot="dbg")
        def attention(n):
                b, c = jobs[n]
                tau0 = 512 * c
                segs = []
                for p in range(4):
                    for e2 in range(2):
                        hs = []
                        for j in range(4):
                            tq = 128 * (4 * c + j)
                            k0 = tq
                            si = 0
                            while k0 < S:
                                wseg = min(1024, S - k0)
                                hs.append([p, e2, j, tq, k0, wseg, si, False])
                                k0 += wseg
                                si += 1
                        hs[-1][7] = True
                        segs.extend(hs)
                pend = []
                prev_ps = None
                prev_w = None
                hcount = [0]

                def stage2(item):
                    (p, e2, j, k0, wseg, gsl, last) = item
                    lo_p, hi_p = 64 * e2, 64 * e2 + 64
                    ts = ts_rr[0] % 2
                    ts_rr[0] += 1
                    tb = ps[:, 2048 + 512 * ts: 2048 + 512 * ts + 512].bitcast(BF16)
                    ak = K(AB[gsl], AB[gsl] + 512)
                    for kk in range(wseg // 128):
                        P.op("pe", I("transpose",
                            out=tb[:, 128 * kk:128 * kk + 128], in_=abuf[gsl][:, 128 * kk:128 * kk + 128], identity=idb[:]),
                            r=ak + ["idb"], w=pbk(4 + ts))
                    nb = wseg // 128
                    kb0 = k0 // 128
                    P.op("act", I("activation",
                        out=AT[:, kb0:kb0 + nb, 128 * j:128 * j + 128],
                        in_=tb[:, 0:wseg].rearrange("p (k t) -> p k t", t=128), func=AF.Copy),
                        r=pbk(4 + ts), w=AT_keys(kb0, nb, j))
                    if last:
                        zs_ = zs_rr[0] % 2
                        zs_rr[0] += 1
                        obk = 2 * zs_
                        ocol = 512 * obk
                        for kb in range(15, 4 * c - 1, -1):
                            i = kb - 4 * c
                            N = 512 if i >= 3 else 128 * (i + 1)
                            P.op("pe", I("matmul",
                                ps[:, ocol:ocol + N], lhsT=vv[:, kb, 128 * p:128 * p + 128], rhs=AT[:, kb, 0:N],
                                start=(kb == 15), stop=(kb == 4 * c)),
                                r=[("v", kb)] + AT_keys_n(kb, N), w=pbk(obk))
                        P.op("dve", I("tensor_tensor",
                            out=ycat[lo_p:hi_p, 4 + p, :], in0=ps[lo_p:hi_p, ocol:ocol + 512], in1=sgs[lo_p:hi_p, p, :], op=ALU.mult),
                            r=pbk(obk) + [("sgs", p)], w=[("yc", 4 + p, e2)])

                for (p, e2, j, tq, k0, wseg, si, last) in segs:
                    lo_p, hi_p = 64 * e2, 64 * e2 + 64
                    zs = zs_rr[0] % 2
                    zs_rr[0] += 1
                    gsl = gs_rr[0] % NG
                    psl = gs_rr[0] % NPB
                    asl = gs_rr[0] % NA
                    gs_rr[0] += 1
                    zcol = 1024 * zs
                    zk = pbk(2 * zs, 2 * zs + 1)
                    npc = (wseg + 511) // 512
                    for i in range(npc):
                        wi = min(512, wseg - 512 * i)
                        first = (si == 0 and i == 0)
                        kchunks = sorted(set([(k0 + 512 * i) // 512, (k0 + 512 * i + wi - 1) // 512]))
                        P.op("pe", I("matmul",
                            ps[:, zcol + 512 * i: zcol + 512 * i + wi], lhsT=qT[lo_p:hi_p, p, tq:tq + 128],
                            rhs=kT[lo_p:hi_p, p, k0 + 512 * i: k0 + 512 * i + wi], start=True, stop=(not first)),
                            r=[("qT", p, c)] + [("kT", p, kc) for kc in kchunks], w=pbk(2 * zs + i))
                        if first:
                            P.op("pe", I("matmul", ps[:, zcol:zcol + 128], lhsT=idb[:], rhs=mkb[:], start=False, stop=True),
                                 r=["idb", "mkb"], w=pbk(2 * zs))
                    gk = K(GB[gsl], GB[gsl] + 1024)
                    pk = K(PB[psl], PB[psl] + 1088)
                    ak = K(AB[asl], AB[asl] + 512)
                    P.op("act", I("activation",
                        out=gbuf[gsl][:, 0:wseg], in_=ps[:, zcol:zcol + wseg], func=AF.Sigmoid, scale=-1.0),
                        r=zk[:npc], w=gk)
                    if si == 0:
                        P.op("dve", I("tensor_tensor_scan",
                            out=pbuf[psl][:, 1:1 + wseg], data0=gbuf[gsl][:, 0:wseg], data1=ps[:, 3584:3585].to_broadcast([128, wseg]),
                            initial=1.0, op0=ALU.mult, op1=ALU.mult), r=gk + [("pb", 7)], w=pk)
                        P.op("pool", I("tensor_scalar", out=abuf[asl][:, 0:1], in0=pbuf[psl][:, 1:2], scalar1=-1.0, scalar2=1.0,
                                       op0=ALU.mult, op1=ALU.add), r=pk, w=ak)
                    else:
                        ppk = K(PB[prev_ps], PB[prev_ps] + 1088)
                        P.op("dve", I("tensor_tensor_scan",
                            out=pbuf[psl][:, 1:1 + wseg], data0=gbuf[gsl][:, 0:wseg], data1=ps[:, 3584:3585].to_broadcast([128, wseg]),
                            initial=pbuf[prev_ps][:, prev_w:prev_w + 1], op0=ALU.mult, op1=ALU.mult), r=gk + ppk + [("pb", 7)], w=pk)
                        P.op("pool", I("tensor_tensor", out=abuf[asl][:, 0:1], in0=pbuf[prev_ps][:, prev_w:prev_w + 1],
                                       in1=pbuf[psl][:, 1:2], op=ALU.subtract), r=pk + ppk, w=ak)
                    P.op("pool", I("tensor_tensor",
                        out=abuf[asl][:, 1:wseg], in0=pbuf[psl][:, 1:wseg], in1=pbuf[psl][:, 2:1 + wseg], op=ALU.subtract),
                        r=pk, w=ak)
                    prev_ps, prev_w = psl, wseg
                    pend.append((p, e2, j, k0, wseg, asl, last))
                    if len(pend) > LAG:
                        stage2(pend.pop(0))
                while pend:
                    stage2(pend.pop(0))

        def p3_reload(n, j):
            b, c = jobs[n]
            row0 = b * S + 512 * c + 128 * j
            xs = 2 + j % 2
            P.op("sp", I("dma_start", out=xt[xs][:], in_=x_d[row0:row0 + 128, :]), w=[("xt", xs)], slot=("x", xs))

        def phase3(n, js=(0, 1, 2, 3)):
                b, c = jobs[n]
                tau0 = 512 * c
                yc_all = [("yc", g) for g in range(4)] + [("yc", 4 + p, e2) for p in range(4) for e2 in range(2)]
                if DEBUG and b == 0 and c == 3:
                    P.op("sp", I("dma_start", out=dbg_d["d_ycat"], in_=ycat[:].rearrange("p g t -> p (g t)")), r=yc_all, slot="dbg")
                for j in js:
                    row0 = b * S + tau0 + 128 * j
                    zs = j % 3
                    zcol = 1024 * zs
                    for half in range(2):
                        for ft in range(8):
                            P.op("pe", I("matmul",
                                ps[:, zcol + 512 * half: zcol + 512 * half + 512], lhsT=ycat[:, ft, 128 * j:128 * j + 128],
                                rhs=w_out_bf[:, ft, 512 * half:512 * half + 512], start=(ft == 0), stop=(ft == 7)),
                                r=yc_all + wout_all, w=pbk(2 * zs + half))
                    yk = pbk(2 * zs, 2 * zs + 1)
                    P.op("act", I("activation", out=junk2, in_=ps[:, zcol:zcol + 1024], func=AF.Square,
                                                                        accum_out=ss2[:, j:j + 1]),
                         r=yk, w=K(JUNK2, JUNK2 + 512) + [("ss2", j)])
                    P.op("pool", I("tensor_scalar", out=rs2[:, j:j + 1], in0=ss2[:, j:j + 1], scalar1=1.0 / D, scalar2=EPS,
                                                                 op0=ALU.mult, op1=ALU.add), r=[("ss2", j)], w=[("rs2", j)])
                    P.op("pool", I("tensor_tensor", out=rs2[:, j:j + 1], in0=rs2[:, j:j + 1], in1=neghalf[:, 0:1], op=ALU.pow),
                         r=[("rs2", j), "neghalf"], w=[("rs2", j)])
                    xs = 2 + j % 2
                    ysi = ys_rr[0] % 2
                    ys_rr[0] += 1
                    ysk = K(YS[ysi], YS[ysi] + 1024)
                    P.op("dve", I("scalar_tensor_tensor",
                        out=ysb[ysi], in0=ps[:, zcol:zcol + 1024], scalar=rs2[:, j:j + 1], in1=gateG[:, b, :],
                        op0=ALU.mult, op1=ALU.mult), r=yk + [("rs2", j), ("gateG", b)], w=ysk)
                    P.op("pool", I("tensor_tensor", out=ysb[ysi], in0=ysb[ysi], in1=xt[xs][:], op=ALU.add),
                         r=ysk + [("xt", xs)], w=ysk)
                    if j + 2 < 4:
                        p3_reload(n, j + 2)
                    P.op("sp", I("dma_start", out=out_d[row0:row0 + 128, :], in_=ysb[ysi]),
                         r=ysk, slot=("o", ysi))
        def phase1_second_half(n, with_elem=True):
            if with_elem:
                p1a_elem(n, 2)
                p1a_elem(n, 3)
            p1a_tr(n, 2)
            p1a_tr(n, 3)
            phase1_rest(n)

        def phase1_tr01(n):
            p1a_tr(n, 0)
            p1a_tr(n, 1)
            p1a_load(n, 2)
            p1a_load(n, 3)

        p1a_load(0, 0)
        p1a_load(0, 1)
        p1a_elem(0, 0)
        p1a_elem(0, 1)
        phase1_tr01(0)
        phase1_second_half(0)
        for n in range(NJ):
            p3_reload(n, 0)
            p3_reload(n, 1)
            if n + 1 < NJ:
                p1a_load(n + 1, 0)
                p1a_load(n + 1, 1)
            attention(n)
            if n + 1 < NJ:
                p1a_elem(n + 1, 0)
                p1a_elem(n + 1, 1)
                phase1_tr01(n + 1)
            phase3(n, (0, 1))
            if n + 1 < NJ:
                p1a_elem(n + 1, 2)
                p1a_elem(n + 1, 3)
            phase3(n, (2, 3))
            if n + 1 < NJ:
                phase1_second_half(n + 1, with_elem=False)
        P.emit(nc)
    return nc


_NC_CACHE = {}


def kernel(x, c, w_ada, b_ada, g_pre, w_in, w_pool, pool_scale, w_out, g_post):
    f32 = np.float32
    x = np.asarray(x, f32)
    c = np.asarray(c, f32)
    w_ada = np.ascontiguousarray(np.asarray(w_ada, f32))
    b_ada = np.asarray(b_ada, f32)
    g_pre = np.asarray(g_pre, f32)
    w_in = np.ascontiguousarray(np.asarray(w_in, f32))
    w_pool = np.ascontiguousarray(np.asarray(w_pool, f32))
    pool_scale = np.asarray(pool_scale, f32)
    w_out = np.ascontiguousarray(np.asarray(w_out, f32))
    g_post = np.asarray(g_post, f32)

    ident = np.eye(128, dtype=f32)
    ii = np.arange(128)
    maskb = np.where(ii[None, :] <= ii[:, None], -30000.0, 0.0).astype(f32)
    cnt = np.ones((4, 16), f32)
    for g in range(4):
        w = 2 << g
        for i in range(w - 1):
            cnt[g, i] = 1.0 / (w - 1 - i)
    cst = np.concatenate([ident, maskb, np.broadcast_to(cnt.reshape(1, 64), (128, 64))], axis=1).astype(f32)
    cols = np.concatenate([g_pre.reshape(8, 128).T, b_ada[:2048].reshape(16, 128).T, pool_scale.reshape(4, 128).T], axis=1)
    cols = np.ascontiguousarray(cols, f32)
    rows = np.ascontiguousarray(np.stack([b_ada[2048:3072], g_post], axis=0), f32)

    in_maps = []
    for i in range(NCORES):
        xc = np.ascontiguousarray(x[BPC * i:BPC * (i + 1), ::-1, :]).reshape(BPC * S, D)
        cc = c[BPC * i:BPC * (i + 1)]
        ct = np.ascontiguousarray(cc.T.reshape(8, 128, BPC).transpose(1, 0, 2).reshape(128, 32), f32)
        in_maps.append({"x": xc, "ct": ct, "cols": cols, "rows": rows, "cst": cst, "w_ada": w_ada,
                        "w_in": w_in, "w_pool": w_pool, "w_out": w_out})
    if "nc" not in _NC_CACHE:
        _NC_CACHE["nc"] = build_nc()
    res = run_bass_kernel_spmd(_NC_CACHE["nc"], in_maps, core_ids=list(range(NCORES)))
    _NC_CACHE["res"] = res
    out = np.empty((NCORES * BPC, S, D), f32)
    for i in range(NCORES):
        o = np.asarray(res.results[i]["out"], f32).reshape(BPC, S, D)
        out[BPC * i:BPC * (i + 1)] = o[:, ::-1, :]
    return out
```

```python
import numpy as np
import ml_dtypes
from contextlib import ExitStack

import concourse.bass as bass
import concourse.mybir as mybir
from concourse.bass_utils import run_bass_kernel_spmd

F32 = mybir.dt.float32
BF16 = mybir.dt.bfloat16
AF = mybir.ActivationFunctionType
ALU = mybir.AluOpType

NCORES = 8
BPC = 4
S = 2048
D = 1024
EPS = 1e-6

ENGS = ("pe", "act", "dve", "pool", "sp")


class _Op:
    __slots__ = ("eng", "fn", "waits", "signal", "count", "dma", "dsem", "dval", "idx")

    def __init__(self, eng, fn, dma):
        self.eng = eng
        self.fn = fn
        self.waits = {}
        self.signal = False
        self.count = None
        self.dma = dma
        self.dsem = None
        self.dval = None


def I(method, *args, **kw):
    return (method, args, kw)


class Prog:
    def __init__(self):
        self.ops = {e: [] for e in ENGS}
        self.last_w = {}
        self.readers = {}
        self.dma_slots = {}

    def op(self, eng, fn, r=(), w=(), slot=None):
        o = _Op(eng, fn, slot is not None)
        if slot is not None:
            self.dma_slots[slot] = self.dma_slots.get(slot, 0) + 16
            o.dsem = slot
            o.dval = self.dma_slots[slot]
        deps = []
        for res in r:
            lw = self.last_w.get(res)
            if lw is not None:
                deps.append((lw, True))
        for res in w:
            lw = self.last_w.get(res)
            if lw is not None:
                deps.append((lw, True))
            for rd in self.readers.get(res, ()):
                deps.append((rd, False))
        ow = o.waits
        for d, raw in deps:
            if d.eng == eng and not d.dma:
                if (not raw) or eng == "pe":
                    continue
            if d.dma:
                key = ("d", d.dsem)
                if ow.get(key, 0) < d.dval:
                    ow[key] = d.dval
            else:
                d.signal = True
                prev = ow.get(d.eng)
                if prev is None or d.idx > prev.idx:
                    ow[d.eng] = d
        for res in r:
            self.readers.setdefault(res, []).append(o)
        for res in w:
            self.last_w[res] = o
            self.readers[res] = []
        o.idx = len(self.ops[eng])
        self.ops[eng].append(o)
        return o

    def emit(self, nc):
        for e in ENGS:
            c = 0
            for o in self.ops[e]:
                if o.signal and not o.dma:
                    c += 1
                    o.count = c
        with ExitStack() as es:
            sems = {e: es.enter_context(nc.semaphore("s_" + e)) for e in ENGS}
            dsems = {s: es.enter_context(nc.semaphore("d_%s" % (s,))) for s in self.dma_slots}
            block = es.enter_context(nc.Block())
            prog = self

            def run(engname, eng):
                waited = {}
                for o in prog.ops[engname]:
                    for key, val in o.waits.items():
                        if isinstance(key, tuple):
                            sem = dsems[key[1]]
                            v = val
                        else:
                            sem = sems[key]
                            v = val.count
                        if waited.get(key, 0) >= v:
                            continue
                        eng.wait_ge(sem, v)
                        waited[key] = v
                    fn = o.fn
                    ins = fn[1](eng) if fn[0] is None else getattr(eng, fn[0])(*fn[1], **fn[2])
                    if o.dma:
                        ins.then_inc(dsems[o.dsem], 16)
                    elif o.signal:
                        ins.then_inc(sems[engname], 1)
                if engname == "sp":
                    for s, v in prog.dma_slots.items():
                        eng.wait_ge(dsems[s], v)

            @block.tensor
            def _(eng):
                run("pe", eng)

            @block.scalar
            def _(eng):
                run("act", eng)

            @block.vector
            def _(eng):
                run("dve", eng)

            @block.gpsimd
            def _(eng):
                run("pool", eng)

            @block.sync
            def _(eng):
                run("sp", eng)


DEBUG = False
SCR_WORDS = 11456
BLK = 64


def build_nc():
    nc = bass.Bass("TRN2", target_bir_lowering=False)
    dt_in = lambda n, s: nc.dram_tensor(n, s, F32, kind="ExternalInput").ap()
    x_d = dt_in("x", [BPC * S, D])
    ct_d = dt_in("ct", [128, 32])
    cols_d = dt_in("cols", [128, 28])
    rows_d = dt_in("rows", [2, 1024])
    cst_d = dt_in("cst", [128, 320])
    wada_d = dt_in("w_ada", [D, 3 * D])
    win_d = dt_in("w_in", [D, 3 * D])
    wpool_d = dt_in("w_pool", [4, 128, 128])
    wout_d = dt_in("w_out", [D, D])
    out_d = nc.dram_tensor("out", [BPC * S, D], F32, kind="ExternalOutput").ap()

    dbg_d = {}
    if DEBUG:
        for n, shp, dt_ in (("d_gmod", [128, 32], F32), ("d_shiftc", [128, 32], F32), ("d_gateG", [128, 4096], F32),
                            ("d_hT", [128, 4096], BF16), ("d_qT", [128, 8192], BF16), ("d_kT", [128, 8192], BF16),
                            ("d_vv", [128, 8192], BF16), ("d_ycat", [128, 4096], BF16), ("d_uT", [128, 2112], F32),
                            ("d_sgp", [128, 2048], BF16), ("d_sgs", [128, 2048], BF16)):
            dbg_d[n] = nc.dram_tensor(n, shp, dt_, kind="ExternalOutput").ap()

    P = Prog()
    with ExitStack() as es:
        sbt = lambda n, s, d: es.enter_context(nc.sbuf_tensor("sb_" + n, s, d))
        w_in_bf = sbt("w_in_bf", [128, 8, 3072], BF16)
        w_out_bf = sbt("w_out_bf", [128, 8, 1024], BF16)
        w_pool_bf = sbt("w_pool_bf", [128, 4, 128], BF16)
        cst = sbt("cst", [128, 320], F32)
        idb = sbt("idb", [128, 128], BF16)
        mkb = sbt("mkb", [128, 128], BF16)
        cols = sbt("cols", [128, 28], F32)
        b1col = sbt("b1col", [128, 8], F32)
        neghalf = sbt("neghalf", [128, 4], F32)
        ctt = sbt("ctt", [128, 32], F32)
        sct = sbt("sct", [128, 32], F32)
        gmod = sbt("gmod", [128, 8, 4], F32)
        shiftc = sbt("shiftc", [128, 8, 4], F32)
        gateG = sbt("gateG", [128, 4, 1024], F32)
        qT = sbt("qT", [128, 4, 2048], BF16)
        kT = sbt("kT", [128, 4, 2048], BF16)
        vv = sbt("vv", [128, 16, 512], BF16)
        halo = sbt("halo", [128, 4, 16], F32)
        xt = [sbt("xt%d" % i, [128, 1024], F32) for i in range(4)]
        sgs = sbt("sgs", [128, 4, 512], BF16)
        ycat = sbt("ycat", [128, 8, 512], BF16)
        ss = sbt("ss", [128, 4], F32)
        rs = sbt("rs", [128, 4], F32)
        ss2 = sbt("ss2", [128, 4], F32)
        rs2 = sbt("rs2", [128, 4], F32)
        etmp = sbt("etmp", [128, 16], F32)
        scr = sbt("scr", [128, SCR_WORDS], F32)
        ps = es.enter_context(nc.psum_tensor("ps", [128, 4096], F32))

        idf = cst[:, 0:128]
        cnt = cst[:, 256:320]

        def K(a, b):
            return [("s", i) for i in range(a // BLK, (b - 1) // BLK + 1)]

        def sview(a, words, bf=False):
            ap = scr[:, a:a + words]
            if bf:
                ap = ap.bitcast(BF16)
            return ap

        def pbk(*banks):
            return [("pb", i) for i in banks]

        WST = [0, 2048]
        REP = 4096
        BGB = 8192
        GPB = 9216
        HT = 0
        UT = 2048
        SA = 4224
        SB = 4752
        PL = [5280, 5536]
        SIG = [5792, 6304]
        SGP = 6816
        JUNK = 7840
        NG, NPB, NA = 2, 3, 4
        LAG = 3
        GB = [i * 1024 for i in range(NG)]
        PB = [2048 + i * 1088 for i in range(NPB)]
        AB = [5312 + i * 512 for i in range(NA)]
        ATO = 7360
        YS = [2048, 3072]
        JUNK2 = 4096

        hT = sview(HT, 2048, True).rearrange("p (k t) -> p k t", k=8)
        uT = sview(UT, 2112).rearrange("p (g t) -> p g t", g=4)
        sA = sview(SA, 528)
        sB = sview(SB, 528)
        pl = [sview(a, 256, True) for a in PL]
        sig = [sview(a, 512) for a in SIG]
        sgp = sview(SGP, 1024, True).rearrange("p (g t) -> p g t", g=4)
        junk = sview(JUNK, 512, True)
        gbuf = [sview(a, 1024) for a in GB]
        pbuf = [sview(a, 1088) for a in PB]
        abuf = [sview(a, 512, True) for a in AB]
        AT = sview(ATO, 4096, True).rearrange("p (k t) -> p k t", k=16)
        ysb = [sview(a, 1024) for a in YS]
        junk2 = sview(JUNK2, 512, True)

        def hT_keys(kt):
            return K(HT + kt * 256, HT + kt * 256 + 256)

        def uT_keys(g):
            return K(UT + g * 528, UT + g * 528 + 528)

        def AT_keys(kb0, nb, j):
            return [("s", (ATO + kb * 256 + j * 64) // BLK) for kb in range(kb0, kb0 + nb)]

        def AT_keys_n(kb, N):
            return [("s", (ATO + kb * 256 + j * 64) // BLK) for j in range(N // 128)]

        cp_rr = [0]

        def copy_any(out, in_, r, w, scale=None, engs=("act", "dve")):
            e = engs[cp_rr[0] % len(engs)]
            cp_rr[0] += 1
            if e == "act":
                if scale is None:
                    P.op("act", I("activation", out=out, in_=in_, func=AF.Copy), r=r, w=w)
                else:
                    P.op("act", I("activation", out=out, in_=in_, func=AF.Copy, scale=scale), r=r, w=w)
            else:
                if scale is None:
                    P.op(e, I("tensor_copy", out=out, in_=in_), r=r, w=w)
                else:
                    P.op(e, I("tensor_scalar", out=out, in0=in_, scalar1=scale, scalar2=None, op0=ALU.mult), r=r, w=w)

        bank_rr = [0]

        def nextbank():
            b = bank_rr[0] % 7
            bank_rr[0] += 1
            return b

        def bank_ap(bk, n=512, off=0):
            return ps[:, bk * 512 + off: bk * 512 + off + n]

        P.op("sp", I("dma_start", out=cst[:], in_=cst_d), w=["cst"], slot="c0")
        P.op("sp", I("dma_start", out=cols[:], in_=cols_d), w=["cols"], slot="c1")
        P.op("sp", I("dma_start", out=ctt[:], in_=ct_d), w=["ctt"], slot="c2")
        bgb = sview(BGB, 1024)
        gpb = sview(GPB, 1024)
        P.op("sp", I("dma_start", out=bgb, in_=rows_d[0].partition_broadcast(128)), w=K(BGB, BGB + 1024), slot="c3")
        P.op("sp", I("dma_start", out=gpb, in_=rows_d[1].partition_broadcast(128)), w=K(GPB, GPB + 1024), slot="c4")
        P.op("dve", I("tensor_copy", out=idb[:], in_=cst[:, 0:128]), r=["cst"], w=["idb"])
        P.op("dve", I("tensor_copy", out=mkb[:], in_=cst[:, 128:256]), r=["cst"], w=["mkb"])
        P.op("pool", I("memset", neghalf[:], -0.5), w=["neghalf"])
        P.op("dve", I("memset", ps[:, 3584:3586], 1.0), w=[("pb", 7)])
        P.op("dve", I("tensor_scalar", out=b1col[:], in0=cols[:, 16:24], scalar1=1.0, scalar2=None, op0=ALU.add), r=["cols"], w=["b1col"])
        P.op("act", I("activation", out=sct[:], in_=ctt[:], func=AF.Sigmoid), r=["ctt"], w=["sct"])
        P.op("dve", I("tensor_tensor", out=sct[:], in0=sct[:], in1=ctt[:], op=ALU.mult), r=["sct", "ctt"], w=["sct"])
        rep = sview(REP, 4096).rearrange("p (k c) -> p k c", k=32)
        P.op("dve", I("tensor_copy", out=rep, in_=sct[:, 0:32].unsqueeze(2).to_broadcast([128, 32, 128])),
             r=["sct"], w=K(REP, REP + 4096))

        wst = [sview(a, 2048).rearrange("p (k f) -> p k f", k=8) for a in WST]
        wst_k = [K(a, a + 2048) for a in WST]
        stg = [0]

        def stage(dram2d, c0):
            i = stg[0] % 2
            stg[0] += 1
            src = dram2d.rearrange("(kt p) f -> p kt f", p=128)[:, :, c0:c0 + 256]
            P.op("sp", I("dma_start", out=wst[i], in_=src), w=wst_k[i], slot=("w", i))
            return i

        for pc in range(12):
            c0 = 256 * pc
            i = stage(wada_d, c0)
            if c0 < 2048:
                for half in range(2):
                    ft = (c0 + 128 * half) // 128
                    bk = nextbank()
                    for kt in range(8):
                        P.op("pe", I("matmul",
                            bank_ap(bk, 4), lhsT=wst[i][:, kt, 128 * half:128 * half + 128],
                            rhs=sct[:, 4 * kt:4 * kt + 4], start=(kt == 0), stop=(kt == 7)),
                            r=wst_k[i] + ["sct"], w=pbk(bk))
                    if ft < 8:
                        P.op("dve", I("tensor_scalar",
                            out=shiftc[:, ft, :], in0=bank_ap(bk, 4), scalar1=cols[:, 8 + ft:9 + ft], scalar2=None, op0=ALU.add),
                            r=pbk(bk) + ["cols"], w=["shiftc"])
                    else:
                        P.op("dve", I("tensor_scalar",
                            out=gmod[:, ft - 8, :], in0=bank_ap(bk, 4), scalar1=b1col[:, ft - 8:ft - 7],
                            scalar2=cols[:, ft - 8:ft - 7], op0=ALU.add, op1=ALU.mult),
                            r=pbk(bk) + ["cols", "b1col"], w=["gmod"])
            else:
                g0 = c0 - 2048
                for b in range(4):
                    bk = nextbank()
                    for kt in range(8):
                        P.op("pe", I("matmul",
                            bank_ap(bk, 256), lhsT=rep[:, 4 * kt + b, :], rhs=wst[i][:, kt, :],
                            start=(kt == 0), stop=(kt == 7)),
                            r=wst_k[i] + K(REP, REP + 4096), w=pbk(bk))
                    P.op("dve", I("tensor_tensor",
                        out=gateG[:, b, g0:g0 + 256], in0=bank_ap(bk, 256), in1=bgb[:, g0:g0 + 256], op=ALU.add),
                        r=pbk(bk) + K(BGB, BGB + 1024), w=[("gateG", b)])
                    P.op("pool", I("tensor_tensor",
                        out=gateG[:, b, g0:g0 + 256], in0=gateG[:, b, g0:g0 + 256], in1=gpb[:, g0:g0 + 256], op=ALU.mult),
                        r=[("gateG", b)] + K(GPB, GPB + 1024), w=[("gateG", b)])
        for pc in range(12):
            i = stage(win_d, 256 * pc)
            copy_any(w_in_bf[:, :, 256 * pc:256 * pc + 256], wst[i], r=wst_k[i], w=[("win", pc)], engs=("dve", "pool", "act"))
        for pc in range(4):
            i = stage(wout_d, 256 * pc)
            copy_any(w_out_bf[:, :, 256 * pc:256 * pc + 256], wst[i], r=wst_k[i], w=[("wout", pc)], engs=("dve", "pool", "act"))
        i = stg[0] % 2
        stg[0] += 1
        wpv = scr[:, WST[i]:WST[i] + 512].rearrange("p (g d) -> p g d", g=4)
        P.op("sp", I("dma_start", out=wpv, in_=wpool_d.rearrange("g c d -> c g d")), w=wst_k[i], slot=("w", i))
        P.op("dve", I("tensor_copy", out=w_pool_bf[:], in_=wpv), r=wst_k[i], w=["wpool"])

        xs_rr = [0]
        zs_rr = [0]
        ts_rr = [0]
        gs_rr = [0]
        ys_rr = [0]
        pl_rr = [0]
        sg_rr = [0]
        win_all = [("win", pc) for pc in range(12)]
        wout_all = [("wout", pc) for pc in range(4)]

        jobs = [(b, c) for b in range(BPC) for c in (3, 2, 1, 0)]
        NJ = len(jobs)

        def p1a_load(n, j):
            b, c = jobs[n]
            row0 = b * S + 512 * c + 128 * j
            xs = j % 2
            P.op("sp", I("dma_start", out=xt[xs][:], in_=x_d[row0:row0 + 128, :]), w=[("xt", xs)], slot=("x", xs))

        def p1a_elem(n, j):
            xs = j % 2
            P.op("act", I("activation", out=junk, in_=xt[xs][:], func=AF.Square, accum_out=ss[:, j:j + 1]),
                 r=[("xt", xs)], w=K(JUNK, JUNK + 512) + [("ss", j)])
            P.op("pool", I("tensor_scalar", out=rs[:, j:j + 1], in0=ss[:, j:j + 1], scalar1=1.0 / D, scalar2=EPS,
                           op0=ALU.mult, op1=ALU.add), r=[("ss", j)], w=[("rs", j)])
            P.op("pool", I("tensor_tensor", out=rs[:, j:j + 1], in0=rs[:, j:j + 1], in1=neghalf[:, 0:1], op=ALU.pow),
                 r=[("rs", j), "neghalf"], w=[("rs", j)])
            P.op("dve", I("tensor_scalar", out=xt[xs][:], in0=xt[xs][:], scalar1=rs[:, j:j + 1], scalar2=None,
                          op0=ALU.mult), r=[("xt", xs), ("rs", j)], w=[("xt", xs)])

        def p1a_tr(n, j):
            b, c = jobs[n]
            xs = j % 2
            p2, jj = j // 2, j % 2
            for k in range(8):
                bk = (k // 2 + 4 * p2) % 7
                P.op("pe", I("transpose", out=bank_ap(bk, 128, (k % 2) * 256 + 128 * jj), in_=xt[xs][:, 128 * k:128 * k + 128],
                             identity=idf), r=[("xt", xs), "cst"], w=pbk(bk))
            if jj == 1:
                for k in range(8):
                    bk = (k // 2 + 4 * p2) % 7
                    src = bank_ap(bk, 256, (k % 2) * 256)
                    dst = hT[:, k, 256 * p2:256 * p2 + 256]
                    if (k // 2) % 2 == 0:
                        P.op("act", I("activation", out=dst, in_=src, func=AF.Identity,
                                      scale=gmod[:, k, b:b + 1], bias=shiftc[:, k, b:b + 1]),
                             r=pbk(bk) + ["gmod", "shiftc"], w=hT_keys(k))
                    else:
                        P.op("dve", I("tensor_scalar", out=dst, in0=src, scalar1=gmod[:, k, b:b + 1],
                                      scalar2=shiftc[:, k, b:b + 1], op0=ALU.mult, op1=ALU.add),
                             r=pbk(bk) + ["gmod", "shiftc"], w=hT_keys(k))

        def phase1_rest(n):
                b, c = jobs[n]
                tau0 = 512 * c
                hT_all = K(HT, HT + 2048)
                if DEBUG and b == 0 and c == 3:
                    P.op("sp", I("dma_start", out=dbg_d["d_hT"], in_=sview(HT, 2048, True)), r=hT_all, slot="dbg")

                def proj_fm(ft):
                    bk = nextbank()
                    for kt in range(8):
                        P.op("pe", I("matmul", bank_ap(bk), lhsT=w_in_bf[:, kt, 128 * ft:128 * ft + 128],
                                                                     rhs=hT[:, kt, :], start=(kt == 0), stop=(kt == 7)),
                             r=[("win", ft // 2)] + hT_keys(kt), w=pbk(bk))
                    return bk

                for g in range(4):
                    bk = proj_fm(g)
                    copy_any(uT[:, g, 0:512], bank_ap(bk), r=pbk(bk), w=uT_keys(g))
                for g in range(4):
                    bk = proj_fm(4 + g)
                    sgi = sg_rr[0] % 2
                    sg_rr[0] += 1
                    P.op("act", I("activation", out=sig[sgi], in_=bank_ap(bk), func=AF.Sigmoid),
                         r=pbk(bk), w=K(SIG[sgi], SIG[sgi] + 512))
                    P.op("dve", I("tensor_tensor", out=sgp[:, g, :], in0=bank_ap(bk), in1=sig[sgi], op=ALU.mult),
                         r=pbk(bk) + K(SIG[sgi], SIG[sgi] + 512), w=K(SGP + 256 * g, SGP + 256 * g + 256))
                def proj_q():
                    for p in range(4):
                        bk = proj_fm(8 + p)
                        copy_any(qT[:, p, tau0:tau0 + 512], bank_ap(bk), r=pbk(bk), w=[("qT", p, c)], scale=0.125)

                def proj_k():
                    for p in range(4):
                        bk = proj_fm(12 + p)
                        copy_any(kT[:, p, tau0:tau0 + 512], bank_ap(bk), r=pbk(bk), w=[("kT", p, c)])

                def proj_gs():
                    for p in range(4):
                        bk = proj_fm(20 + p)
                        sgi = sg_rr[0] % 2
                        sg_rr[0] += 1
                        P.op("act", I("activation", out=sig[sgi], in_=bank_ap(bk), func=AF.Sigmoid),
                             r=pbk(bk), w=K(SIG[sgi], SIG[sgi] + 512))
                        P.op("dve", I("tensor_tensor", out=sgs[:, p, :], in0=bank_ap(bk), in1=sig[sgi], op=ALU.mult),
                             r=pbk(bk) + K(SIG[sgi], SIG[sgi] + 512), w=[("sgs", p)])

                def proj_v():
                    for j in range(4):
                        bk = nextbank()
                        blk = 4 * c + j
                        for kt in range(8):
                            P.op("pe", I("matmul", bank_ap(bk), lhsT=hT[:, kt, 128 * j:128 * j + 128],
                                         rhs=w_in_bf[:, kt, 2048:2560], start=(kt == 0), stop=(kt == 7)),
                                 r=[("win", 8), ("win", 9)] + hT_keys(kt), w=pbk(bk))
                        copy_any(vv[:, blk, :], bank_ap(bk), r=pbk(bk), w=[("v", blk)])

                def pool_elem(g):
                    w = 2 << g
                    if c == 3:
                        P.op("pool", I("memset", uT[:, g, 512:528], 0.0), w=K(UT + g * 528 + 512, UT + g * 528 + 528))
                    else:
                        P.op("pool", I("tensor_copy", out=uT[:, g, 512:528], in_=halo[:, g, :]),
                             r=[("halo", g)], w=K(UT + g * 528 + 512, UT + g * 528 + 528))
                    P.op("pool", I("tensor_tensor", out=sA[:, 0:527], in0=uT[:, g, 0:527], in1=uT[:, g, 1:528], op=ALU.add),
                         r=uT_keys(g), w=K(SA, SA + 528))
                    cur, curk, oth, othk = sA, K(SA, SA + 528), sB, K(SB, SB + 528)
                    ln = 527
                    step = 2
                    while step < w:
                        nl = ln - step
                        P.op("pool", I("tensor_tensor",
                            out=oth[:, 0:nl], in0=cur[:, 0:nl], in1=cur[:, step:step + nl], op=ALU.add), r=curk, w=othk)
                        cur, curk, oth, othk = oth, othk, cur, curk
                        ln = nl
                        step *= 2
                    pli = pl_rr[0] % 2
                    pl_rr[0] += 1
                    plk = K(PL[pli], PL[pli] + 256)
                    P.op("dve", I("scalar_tensor_tensor",
                        out=pl[pli], in0=cur[:, 0:512], scalar=1.0 / w, in1=uT[:, g, 0:512], op0=ALU.mult, op1=ALU.subtract),
                        r=curk + uT_keys(g), w=plk)
                    if c == 3:
                        lo = 513 - w
                        P.op("dve", I("tensor_tensor",
                            out=etmp[:, 0:w - 1], in0=cur[:, lo:512], in1=cnt[:, 16 * g:16 * g + w - 1], op=ALU.mult),
                            r=curk + ["cst"], w=["etmp"])
                        P.op("dve", I("tensor_tensor",
                            out=pl[pli][:, lo:512], in0=etmp[:, 0:w - 1], in1=uT[:, g, lo:512], op=ALU.subtract),
                            r=["etmp"] + uT_keys(g), w=plk)
                    P.op("pool", I("tensor_copy", out=halo[:, g, :], in_=uT[:, g, 0:16]), r=uT_keys(g), w=[("halo", g)])
                    return pli, plk

                def pool_mm(g, pli, plk):
                    bk = nextbank()
                    P.op("pe", I("matmul", bank_ap(bk), lhsT=w_pool_bf[:, g, :], rhs=pl[pli], start=True, stop=True),
                         r=["wpool"] + plk, w=pbk(bk))
                    P.op("dve", I("scalar_tensor_tensor",
                        out=ycat[:, g, :], in0=bank_ap(bk), scalar=cols[:, 24 + g:25 + g], in1=sgp[:, g, :], op0=ALU.mult, op1=ALU.mult),
                        r=pbk(bk) + ["cols"] + K(SGP + 256 * g, SGP + 256 * g + 256), w=[("yc", g)])

                pe0 = pool_elem(0)
                proj_q()
                pool_mm(0, *pe0)
                pe1 = pool_elem(1)
                proj_k()
                pool_mm(1, *pe1)
                pe2 = pool_elem(2)
                proj_gs()
                pool_mm(2, *pe2)
                pe3 = pool_elem(3)
                proj_v()
                pool_mm(3, *pe3)

                if DEBUG and b == 0 and c == 3:
                    P.op("sp", I("dma_start", out=dbg_d["d_uT"], in_=sview(UT, 2112)), r=K(UT, UT + 2112), slot="dbg")
                    P.op("sp", I("dma_start", out=dbg_d["d_sgp"], in_=sview(SGP, 1024, True)), r=K(SGP, SGP + 1024), slot="dbg")
                    P.op("sp", I("dma_start", out=dbg_d["d_sgs"], in_=sgs[:].rearrange("p g t -> p (g t)")), r=[("sgs", p) for p in range(4)], slot="dbg")
                if DEBUG and b == 0 and c == 0:
                    P.op("sp", I("dma_start", out=dbg_d["d_qT"], in_=qT[:].rearrange("p g t -> p (g t)")), r=[("qT", p, cc) for p in range(4) for cc in range(4)], slot="dbg")
                    P.op("sp", I("dma_start", out=dbg_d["d_kT"], in_=kT[:].rearrange("p g t -> p (g t)")), r=[("kT", p, cc) for p in range(4) for cc in range(4)], slot="dbg")
                    P.op("sp", I("dma_start", out=dbg_d["d_vv"], in_=vv[:].rearrange("p g t -> p (g t)")), r=[("v", kb) for kb in range(16)], slot="dbg")
                    P.op("sp", I("dma_start", out=dbg_d["d_gmod"], in_=gmod[:].rearrange("p g t -> p (g t)")), r=["gmod"], slot="dbg")
                    P.op("sp", I("dma_start", out=dbg_d["d_shiftc"], in_=shiftc[:].rearrange("p g t -> p (g t)")), r=["shiftc"], slot="dbg")
                    P.op("sp", I("dma_start", out=dbg_d["d_gateG"], in_=gateG[:].rearrange("p g t -> p (g t)")), r=[("gateG", bb) for bb in range(4)], slot="dbg")
        def attention(n):
                b, c = jobs[n]
                tau0 = 512 * c
                segs = []
                for p in range(4):
                    for e2 in range(2):
                        hs = []
                        for j in range(4):
                            tq = 128 * (4 * c + j)
                            k0 = tq
                            si = 0
                            while k0 < S:
                                wseg = min(1024, S - k0)
                                hs.append([p, e2, j, tq, k0, wseg, si, False])
                                k0 += wseg
                                si += 1
                        hs[-1][7] = True
                        segs.extend(hs)
                pend = []
                prev_ps = None
                prev_w = None
                hcount = [0]

                def stage2(item):
                    (p, e2, j, k0, wseg, gsl, last) = item
                    lo_p, hi_p = 64 * e2, 64 * e2 + 64
                    ts = ts_rr[0] % 2
                    ts_rr[0] += 1
                    tb = ps[:, 2048 + 512 * ts: 2048 + 512 * ts + 512].bitcast(BF16)
                    ak = K(AB[gsl], AB[gsl] + 512)
                    for kk in range(wseg // 128):
                        P.op("pe", I("transpose",
                            out=tb[:, 128 * kk:128 * kk + 128], in_=abuf[gsl][:, 128 * kk:128 * kk + 128], identity=idb[:]),
                            r=ak + ["idb"], w=pbk(4 + ts))
                    nb = wseg // 128
                    kb0 = k0 // 128
                    P.op("act", I("activation",
                        out=AT[:, kb0:kb0 + nb, 128 * j:128 * j + 128],
                        in_=tb[:, 0:wseg].rearrange("p (k t) -> p k t", t=128), func=AF.Copy),
                        r=pbk(4 + ts), w=AT_keys(kb0, nb, j))
                    if last:
                        zs_ = zs_rr[0] % 2
                        zs_rr[0] += 1
                        obk = 2 * zs_
                        ocol = 512 * obk
                        for kb in range(15, 4 * c - 1, -1):
                            i = kb - 4 * c
                            N = 512 if i >= 3 else 128 * (i + 1)
                            P.op("pe", I("matmul",
                                ps[:, ocol:ocol + N], lhsT=vv[:, kb, 128 * p:128 * p + 128], rhs=AT[:, kb, 0:N],
                                start=(kb == 15), stop=(kb == 4 * c)),
                                r=[("v", kb)] + AT_keys_n(kb, N), w=pbk(obk))
                        P.op("dve", I("tensor_tensor",
                            out=ycat[lo_p:hi_p, 4 + p, :], in0=ps[lo_p:hi_p, ocol:ocol + 512], in1=sgs[lo_p:hi_p, p, :], op=ALU.mult),
                            r=pbk(obk) + [("sgs", p)], w=[("yc", 4 + p, e2)])

                for (p, e2, j, tq, k0, wseg, si, last) in segs:
                    lo_p, hi_p = 64 * e2, 64 * e2 + 64
                    zs = zs_rr[0] % 2
                    zs_rr[0] += 1
                    gsl = gs_rr[0] % NG
                    psl = gs_rr[0] % NPB
                    asl = gs_rr[0] % NA
                    gs_rr[0] += 1
                    zcol = 1024 * zs
                    zk = pbk(2 * zs, 2 * zs + 1)
                    npc = (wseg + 511) // 512
                    for i in range(npc):
                        wi = min(512, wseg - 512 * i)
                        first = (si == 0 and i == 0)
                        kchunks = sorted(set([(k0 + 512 * i) // 512, (k0 + 512 * i + wi - 1) // 512]))
                        P.op("pe", I("matmul",
                            ps[:, zcol + 512 * i: zcol + 512 * i + wi], lhsT=qT[lo_p:hi_p, p, tq:tq + 128],
                            rhs=kT[lo_p:hi_p, p, k0 + 512 * i: k0 + 512 * i + wi], start=True, stop=(not first)),
                            r=[("qT", p, c)] + [("kT", p, kc) for kc in kchunks], w=pbk(2 * zs + i))
                        if first:
                            P.op("pe", I("matmul", ps[:, zcol:zcol + 128], lhsT=idb[:], rhs=mkb[:], start=False, stop=True),
                                 r=["idb", "mkb"], w=pbk(2 * zs))
                    nd_ = min(512, wseg)
                    for _d in range(2):
                        P.op("pe", I("matmul", ps[:, 3072:3072 + nd_], lhsT=idb[:], rhs=qT[:, p, tau0:tau0 + nd_], start=True, stop=True),
                             r=["idb", ("qT", p, c)], w=pbk(6))
                    gk = K(GB[gsl], GB[gsl] + 1024)
                    pk = K(PB[psl], PB[psl] + 1088)
                    ak = K(AB[asl], AB[asl] + 512)
                    P.op("act", I("activation",
                        out=gbuf[gsl][:, 0:wseg], in_=ps[:, zcol:zcol + wseg], func=AF.Sigmoid, scale=-1.0),
                        r=zk[:npc], w=gk)
                    if si == 0:
                        P.op("dve", I("tensor_tensor_scan",
                            out=pbuf[psl][:, 1:1 + wseg], data0=gbuf[gsl][:, 0:wseg], data1=ps[:, 3584:3585].to_broadcast([128, wseg]),
                            initial=1.0, op0=ALU.mult, op1=ALU.mult), r=gk + [("pb", 7)], w=pk)
                        P.op("pool", I("tensor_scalar", out=abuf[asl][:, 0:1], in0=pbuf[psl][:, 1:2], scalar1=-1.0, scalar2=1.0,
                                       op0=ALU.mult, op1=ALU.add), r=pk, w=ak)
                    else:
                        ppk = K(PB[prev_ps], PB[prev_ps] + 1088)
                        P.op("dve", I("tensor_tensor_scan",
                            out=pbuf[psl][:, 1:1 + wseg], data0=gbuf[gsl][:, 0:wseg], data1=ps[:, 3584:3585].to_broadcast([128, wseg]),
                            initial=pbuf[prev_ps][:, prev_w:prev_w + 1], op0=ALU.mult, op1=ALU.mult), r=gk + ppk + [("pb", 7)], w=pk)
                        P.op("pool", I("tensor_tensor", out=abuf[asl][:, 0:1], in0=pbuf[prev_ps][:, prev_w:prev_w + 1],
                                       in1=pbuf[psl][:, 1:2], op=ALU.subtract), r=pk + ppk, w=ak)
                    P.op("pool", I("tensor_tensor",
                        out=abuf[asl][:, 1:wseg], in0=pbuf[psl][:, 1:wseg], in1=pbuf[psl][:, 2:1 + wseg], op=ALU.subtract),
                        r=pk, w=ak)
                    prev_ps, prev_w = psl, wseg
                    pend.append((p, e2, j, k0, wseg, asl, last))
                    if len(pend) > LAG:
                        stage2(pend.pop(0))
                while pend:
                    stage2(pend.pop(0))

        def p3_reload(n, j):
            b, c = jobs[n]
            row0 = b * S + 512 * c + 128 * j
            xs = 2 + j % 2
            P.op("sp", I("dma_start", out=xt[xs][:], in_=x_d[row0:row0 + 128, :]), w=[("xt", xs)], slot=("x", xs))

        def phase3(n, js=(0, 1, 2, 3)):
                b, c = jobs[n]
                tau0 = 512 * c
                yc_all = [("yc", g) for g in range(4)] + [("yc", 4 + p, e2) for p in range(4) for e2 in range(2)]
                if DEBUG and b == 0 and c == 3:
                    P.op("sp", I("dma_start", out=dbg_d["d_ycat"], in_=ycat[:].rearrange("p g t -> p (g t)")), r=yc_all, slot="dbg")
                for j in js:
                    row0 = b * S + tau0 + 128 * j
                    zs = j % 3
                    zcol = 1024 * zs
                    for half in range(2):
                        for ft in range(8):
                            P.op("pe", I("matmul",
                                ps[:, zcol + 512 * half: zcol + 512 * half + 512], lhsT=ycat[:, ft, 128 * j:128 * j + 128],
                                rhs=w_out_bf[:, ft, 512 * half:512 * half + 512], start=(ft == 0), stop=(ft == 7)),
                                r=yc_all + wout_all, w=pbk(2 * zs + half))
                    yk = pbk(2 * zs, 2 * zs + 1)
                    P.op("act", I("activation", out=junk2, in_=ps[:, zcol:zcol + 1024], func=AF.Square,
                                                                        accum_out=ss2[:, j:j + 1]),
                         r=yk, w=K(JUNK2, JUNK2 + 512) + [("ss2", j)])
                    P.op("pool", I("tensor_scalar", out=rs2[:, j:j + 1], in0=ss2[:, j:j + 1], scalar1=1.0 / D, scalar2=EPS,
                                                                 op0=ALU.mult, op1=ALU.add), r=[("ss2", j)], w=[("rs2", j)])
                    P.op("pool", I("tensor_tensor", out=rs2[:, j:j + 1], in0=rs2[:, j:j + 1], in1=neghalf[:, 0:1], op=ALU.pow),
                         r=[("rs2", j), "neghalf"], w=[("rs2", j)])
                    xs = 2 + j % 2
                    ysi = ys_rr[0] % 2
                    ys_rr[0] += 1
                    ysk = K(YS[ysi], YS[ysi] + 1024)
                    P.op("dve", I("scalar_tensor_tensor",
                        out=ysb[ysi], in0=ps[:, zcol:zcol + 1024], scalar=rs2[:, j:j + 1], in1=gateG[:, b, :],
                        op0=ALU.mult, op1=ALU.mult), r=yk + [("rs2", j), ("gateG", b)], w=ysk)
                    P.op("pool", I("tensor_tensor", out=ysb[ysi], in0=ysb[ysi], in1=xt[xs][:], op=ALU.add),
                         r=ysk + [("xt", xs)], w=ysk)
                    if j + 2 < 4:
                        p3_reload(n, j + 2)
                    P.op("sp", I("dma_start", out=out_d[row0:row0 + 128, :], in_=ysb[ysi]),
                         r=ysk, slot=("o", ysi))
        def phase1_second_half(n, with_elem=True):
            if with_elem:
                p1a_elem(n, 2)
                p1a_elem(n, 3)
            p1a_tr(n, 2)
            p1a_tr(n, 3)
            phase1_rest(n)

        def phase1_tr01(n):
            p1a_tr(n, 0)
            p1a_tr(n, 1)
            p1a_load(n, 2)
            p1a_load(n, 3)

        p1a_load(0, 0)
        p1a_load(0, 1)
        p1a_elem(0, 0)
        p1a_elem(0, 1)
        phase1_tr01(0)
        phase1_second_half(0)
        for n in range(NJ):
            p3_reload(n, 0)
            p3_reload(n, 1)
            if n + 1 < NJ:
                p1a_load(n + 1, 0)
                p1a_load(n + 1, 1)
            attention(n)
            if n + 1 < NJ:
                p1a_elem(n + 1, 0)
                p1a_elem(n + 1, 1)
                phase1_tr01(n + 1)
            phase3(n, (0, 1))
            if n + 1 < NJ:
                p1a_elem(n + 1, 2)
                p1a_elem(n + 1, 3)
            phase3(n, (2, 3))
            if n + 1 < NJ:
                phase1_second_half(n + 1, with_elem=False)
        P.emit(nc)
    return nc


_NC_CACHE = {}


def kernel(x, c, w_ada, b_ada, g_pre, w_in, w_pool, pool_scale, w_out, g_post):
    f32 = np.float32
    x = np.asarray(x, f32)
    c = np.asarray(c, f32)
    w_ada = np.ascontiguousarray(np.asarray(w_ada, f32))
    b_ada = np.asarray(b_ada, f32)
    g_pre = np.asarray(g_pre, f32)
    w_in = np.ascontiguousarray(np.asarray(w_in, f32))
    w_pool = np.ascontiguousarray(np.asarray(w_pool, f32))
    pool_scale = np.asarray(pool_scale, f32)
    w_out = np.ascontiguousarray(np.asarray(w_out, f32))
    g_post = np.asarray(g_post, f32)

    ident = np.eye(128, dtype=f32)
    ii = np.arange(128)
    maskb = np.where(ii[None, :] <= ii[:, None], -30000.0, 0.0).astype(f32)
    cnt = np.ones((4, 16), f32)
    for g in range(4):
        w = 2 << g
        for i in range(w - 1):
            cnt[g, i] = 1.0 / (w - 1 - i)
    cst = np.concatenate([ident, maskb, np.broadcast_to(cnt.reshape(1, 64), (128, 64))], axis=1).astype(f32)
    cols = np.concatenate([g_pre.reshape(8, 128).T, b_ada[:2048].reshape(16, 128).T, pool_scale.reshape(4, 128).T], axis=1)
    cols = np.ascontiguousarray(cols, f32)
    rows = np.ascontiguousarray(np.stack([b_ada[2048:3072], g_post], axis=0), f32)

    in_maps = []
    for i in range(NCORES):
        xc = np.ascontiguousarray(x[BPC * i:BPC * (i + 1), ::-1, :]).reshape(BPC * S, D)
        cc = c[BPC * i:BPC * (i + 1)]
        ct = np.ascontiguousarray(cc.T.reshape(8, 128, BPC).transpose(1, 0, 2).reshape(128, 32), f32)
        in_maps.append({"x": xc, "ct": ct, "cols": cols, "rows": rows, "cst": cst, "w_ada": w_ada,
                        "w_in": w_in, "w_pool": w_pool, "w_out": w_out})
    if "nc" not in _NC_CACHE:
        _NC_CACHE["nc"] = build_nc()
    res = run_bass_kernel_spmd(_NC_CACHE["nc"], in_maps, core_ids=list(range(NCORES)))
    _NC_CACHE["res"] = res
    out = np.empty((NCORES * BPC, S, D), f32)
    for i in range(NCORES):
        o = np.asarray(res.results[i]["out"], f32).reshape(BPC, S, D)
        out[BPC * i:BPC * (i + 1)] = o[:, ::-1, :]
    return out
```

```python
import numpy as np
import ml_dtypes
from contextlib import ExitStack

import concourse.bass as bass
import concourse.mybir as mybir
from concourse.bass_utils import run_bass_kernel_spmd

F32 = mybir.dt.float32
BF16 = mybir.dt.bfloat16
AF = mybir.ActivationFunctionType
ALU = mybir.AluOpType

NCORES = 8
BPC = 4
S = 2048
D = 1024
EPS = 1e-6

ENGS = ("pe", "act", "dve", "pool", "sp")


class _Op:
    __slots__ = ("eng", "fn", "waits", "signal", "count", "dma", "dsem", "dval", "idx")

    def __init__(self, eng, fn, dma):
        self.eng = eng
        self.fn = fn
        self.waits = {}
        self.signal = False
        self.count = None
        self.dma = dma
        self.dsem = None
        self.dval = None


def I(method, *args, **kw):
    return (method, args, kw)


class Prog:
    def __init__(self):
        self.ops = {e: [] for e in ENGS}
        self.last_w = {}
        self.readers = {}
        self.dma_slots = {}

    def op(self, eng, fn, r=(), w=(), slot=None):
        o = _Op(eng, fn, slot is not None)
        if slot is not None:
            self.dma_slots[slot] = self.dma_slots.get(slot, 0) + 16
            o.dsem = slot
            o.dval = self.dma_slots[slot]
        deps = []
        for res in r:
            lw = self.last_w.get(res)
            if lw is not None:
                deps.append((lw, True))
        for res in w:
            lw = self.last_w.get(res)
            if lw is not None:
                deps.append((lw, True))
            for rd in self.readers.get(res, ()):
                deps.append((rd, False))
        ow = o.waits
        for d, raw in deps:
            if d.eng == eng and not d.dma:
                if (not raw) or eng == "pe":
                    continue
            if d.dma:
                key = ("d", d.dsem)
                if ow.get(key, 0) < d.dval:
                    ow[key] = d.dval
            else:
                d.signal = True
                prev = ow.get(d.eng)
                if prev is None or d.idx > prev.idx:
                    ow[d.eng] = d
        for res in r:
            self.readers.setdefault(res, []).append(o)
        for res in w:
            self.last_w[res] = o
            self.readers[res] = []
        o.idx = len(self.ops[eng])
        self.ops[eng].append(o)
        return o

    def emit(self, nc):
        for e in ENGS:
            c = 0
            for o in self.ops[e]:
                if o.signal and not o.dma:
                    c += 1
                    o.count = c
        with ExitStack() as es:
            sems = {e: es.enter_context(nc.semaphore("s_" + e)) for e in ENGS}
            dsems = {s: es.enter_context(nc.semaphore("d_%s" % (s,))) for s in self.dma_slots}
            block = es.enter_context(nc.Block())
            prog = self

            def run(engname, eng):
                waited = {}
                for o in prog.ops[engname]:
                    for key, val in o.waits.items():
                        if isinstance(key, tuple):
                            sem = dsems[key[1]]
                            v = val
                        else:
                            sem = sems[key]
                            v = val.count
                        if waited.get(key, 0) >= v:
                            continue
                        eng.wait_ge(sem, v)
                        waited[key] = v
                    fn = o.fn
                    ins = fn[1](eng) if fn[0] is None else getattr(eng, fn[0])(*fn[1], **fn[2])
                    if o.dma:
                        ins.then_inc(dsems[o.dsem], 16)
                    elif o.signal:
                        ins.then_inc(sems[engname], 1)
                if engname == "sp":
                    for s, v in prog.dma_slots.items():
                        eng.wait_ge(dsems[s], v)

            @block.tensor
            def _(eng):
                run("pe", eng)

            @block.scalar
            def _(eng):
                run("act", eng)

            @block.vector
            def _(eng):
                run("dve", eng)

            @block.gpsimd
            def _(eng):
                run("pool", eng)

            @block.sync
            def _(eng):
                run("sp", eng)


DEBUG = False
SCR_WORDS = 11456
BLK = 64


def build_nc():
    nc = bass.Bass("TRN2", target_bir_lowering=False)
    dt_in = lambda n, s: nc.dram_tensor(n, s, F32, kind="ExternalInput").ap()
    x_d = dt_in("x", [BPC * S, D])
    ct_d = dt_in("ct", [128, 32])
    cols_d = dt_in("cols", [128, 28])
    rows_d = dt_in("rows", [2, 1024])
    cst_d = dt_in("cst", [128, 320])
    wada_d = dt_in("w_ada", [D, 3 * D])
    win_d = dt_in("w_in", [D, 3 * D])
    wpool_d = dt_in("w_pool", [4, 128, 128])
    wout_d = dt_in("w_out", [D, D])
    out_d = nc.dram_tensor("out", [BPC * S, D], F32, kind="ExternalOutput").ap()

    dbg_d = {}
    if DEBUG:
        for n, shp, dt_ in (("d_gmod", [128, 32], F32), ("d_shiftc", [128, 32], F32), ("d_gateG", [128, 4096], F32),
                            ("d_hT", [128, 4096], BF16), ("d_qT", [128, 8192], BF16), ("d_kT", [128, 8192], BF16),
                            ("d_vv", [128, 8192], BF16), ("d_ycat", [128, 4096], BF16), ("d_uT", [128, 2112], F32),
                            ("d_sgp", [128, 2048], BF16), ("d_sgs", [128, 2048], BF16)):
            dbg_d[n] = nc.dram_tensor(n, shp, dt_, kind="ExternalOutput").ap()

    P = Prog()
    with ExitStack() as es:
        sbt = lambda n, s, d: es.enter_context(nc.sbuf_tensor("sb_" + n, s, d))
        w_in_bf = sbt("w_in_bf", [128, 8, 3072], BF16)
        w_out_bf = sbt("w_out_bf", [128, 8, 1024], BF16)
        w_pool_bf = sbt("w_pool_bf", [128, 4, 128], BF16)
        cst = sbt("cst", [128, 320], F32)
        idb = sbt("idb", [128, 128], BF16)
        mkb = sbt("mkb", [128, 128], BF16)
        cols = sbt("cols", [128, 28], F32)
        b1col = sbt("b1col", [128, 8], F32)
        neghalf = sbt("neghalf", [128, 4], F32)
        ctt = sbt("ctt", [128, 32], F32)
        sct = sbt("sct", [128, 32], F32)
        gmod = sbt("gmod", [128, 8, 4], F32)
        shiftc = sbt("shiftc", [128, 8, 4], F32)
        gateG = sbt("gateG", [128, 4, 1024], F32)
        qT = sbt("qT", [128, 4, 2048], BF16)
        kT = sbt("kT", [128, 4, 2048], BF16)
        vv = sbt("vv", [128, 16, 512], BF16)
        halo = sbt("halo", [128, 4, 16], F32)
        xt = [sbt("xt%d" % i, [128, 1024], F32) for i in range(4)]
        sgs = sbt("sgs", [128, 4, 512], BF16)
        ycat = sbt("ycat", [128, 8, 512], BF16)
        ss = sbt("ss", [128, 4], F32)
        rs = sbt("rs", [128, 4], F32)
        ss2 = sbt("ss2", [128, 4], F32)
        rs2 = sbt("rs2", [128, 4], F32)
        etmp = sbt("etmp", [128, 16], F32)
        scr = sbt("scr", [128, SCR_WORDS], F32)
        ps = es.enter_context(nc.psum_tensor("ps", [128, 4096], F32))

        idf = cst[:, 0:128]
        cnt = cst[:, 256:320]

        def K(a, b):
            return [("s", i) for i in range(a // BLK, (b - 1) // BLK + 1)]

        def sview(a, words, bf=False):
            ap = scr[:, a:a + words]
            if bf:
                ap = ap.bitcast(BF16)
            return ap

        def pbk(*banks):
            return [("pb", i) for i in banks]

        WST = [0, 2048]
        REP = 4096
        BGB = 8192
        GPB = 9216
        HT = 0
        UT = 2048
        SA = 4224
        SB = 4752
        PL = [5280, 5536]
        SIG = [5792, 6304]
        SGP = 6816
        JUNK = 7840
        NG, NPB, NA = 2, 3, 4
        LAG = 3
        GB = [i * 1024 for i in range(NG)]
        PB = [2048 + i * 1088 for i in range(NPB)]
        AB = [5312 + i * 512 for i in range(NA)]
        ATO = 7360
        YS = [2048, 3072]
        JUNK2 = 4096

        hT = sview(HT, 2048, True).rearrange("p (k t) -> p k t", k=8)
        uT = sview(UT, 2112).rearrange("p (g t) -> p g t", g=4)
        sA = sview(SA, 528)
        sB = sview(SB, 528)
        pl = [sview(a, 256, True) for a in PL]
        sig = [sview(a, 512) for a in SIG]
        sgp = sview(SGP, 1024, True).rearrange("p (g t) -> p g t", g=4)
        junk = sview(JUNK, 512, True)
        gbuf = [sview(a, 1024) for a in GB]
        pbuf = [sview(a, 1088) for a in PB]
        abuf = [sview(a, 512, True) for a in AB]
        AT = sview(ATO, 4096, True).rearrange("p (k t) -> p k t", k=16)
        ysb = [sview(a, 1024) for a in YS]
        junk2 = sview(JUNK2, 512, True)

        def hT_keys(kt):
            return K(HT + kt * 256, HT + kt * 256 + 256)

        def uT_keys(g):
            return K(UT + g * 528, UT + g * 528 + 528)

        def AT_keys(kb0, nb, j):
            return [("s", (ATO + kb * 256 + j * 64) // BLK) for kb in range(kb0, kb0 + nb)]

        def AT_keys_n(kb, N):
            return [("s", (ATO + kb * 256 + j * 64) // BLK) for j in range(N // 128)]

        cp_rr = [0]

        def copy_any(out, in_, r, w, scale=None, engs=("act", "dve")):
            e = engs[cp_rr[0] % len(engs)]
            cp_rr[0] += 1
            if e == "act":
                if scale is None:
                    P.op("act", I("activation", out=out, in_=in_, func=AF.Copy), r=r, w=w)
                else:
                    P.op("act", I("activation", out=out, in_=in_, func=AF.Copy, scale=scale), r=r, w=w)
            else:
                if scale is None:
                    P.op(e, I("tensor_copy", out=out, in_=in_), r=r, w=w)
                else:
                    P.op(e, I("tensor_scalar", out=out, in0=in_, scalar1=scale, scalar2=None, op0=ALU.mult), r=r, w=w)

        bank_rr = [0]

        def nextbank():
            b = bank_rr[0] % 7
            bank_rr[0] += 1
            return b

        def bank_ap(bk, n=512, off=0):
            return ps[:, bk * 512 + off: bk * 512 + off + n]

        P.op("sp", I("dma_start", out=cst[:], in_=cst_d), w=["cst"], slot="c0")
        P.op("sp", I("dma_start", out=cols[:], in_=cols_d), w=["cols"], slot="c1")
        P.op("sp", I("dma_start", out=ctt[:], in_=ct_d), w=["ctt"], slot="c2")
        bgb = sview(BGB, 1024)
        gpb = sview(GPB, 1024)
        P.op("sp", I("dma_start", out=bgb, in_=rows_d[0].partition_broadcast(128)), w=K(BGB, BGB + 1024), slot="c3")
        P.op("sp", I("dma_start", out=gpb, in_=rows_d[1].partition_broadcast(128)), w=K(GPB, GPB + 1024), slot="c4")
        P.op("dve", I("tensor_copy", out=idb[:], in_=cst[:, 0:128]), r=["cst"], w=["idb"])
        P.op("dve", I("tensor_copy", out=mkb[:], in_=cst[:, 128:256]), r=["cst"], w=["mkb"])
        P.op("pool", I("memset", neghalf[:], -0.5), w=["neghalf"])
        P.op("dve", I("memset", ps[:, 3584:3586], 1.0), w=[("pb", 7)])
        P.op("dve", I("tensor_scalar", out=b1col[:], in0=cols[:, 16:24], scalar1=1.0, scalar2=None, op0=ALU.add), r=["cols"], w=["b1col"])
        P.op("act", I("activation", out=sct[:], in_=ctt[:], func=AF.Sigmoid), r=["ctt"], w=["sct"])
        P.op("dve", I("tensor_tensor", out=sct[:], in0=sct[:], in1=ctt[:], op=ALU.mult), r=["sct", "ctt"], w=["sct"])
        rep = sview(REP, 4096).rearrange("p (k c) -> p k c", k=32)
        P.op("dve", I("tensor_copy", out=rep, in_=sct[:, 0:32].unsqueeze(2).to_broadcast([128, 32, 128])),
             r=["sct"], w=K(REP, REP + 4096))

        wst = [sview(a, 2048).rearrange("p (k f) -> p k f", k=8) for a in WST]
        wst_k = [K(a, a + 2048) for a in WST]
        stg = [0]

        def stage(dram2d, c0):
            i = stg[0] % 2
            stg[0] += 1
            src = dram2d.rearrange("(kt p) f -> p kt f", p=128)[:, :, c0:c0 + 256]
            P.op("sp", I("dma_start", out=wst[i], in_=src), w=wst_k[i], slot=("w", i))
            return i

        for pc in range(12):
            c0 = 256 * pc
            i = stage(wada_d, c0)
            if c0 < 2048:
                for half in range(2):
                    ft = (c0 + 128 * half) // 128
                    bk = nextbank()
                    for kt in range(8):
                        P.op("pe", I("matmul",
                            bank_ap(bk, 4), lhsT=wst[i][:, kt, 128 * half:128 * half + 128],
                            rhs=sct[:, 4 * kt:4 * kt + 4], start=(kt == 0), stop=(kt == 7)),
                            r=wst_k[i] + ["sct"], w=pbk(bk))
                    if ft < 8:
                        P.op("dve", I("tensor_scalar",
                            out=shiftc[:, ft, :], in0=bank_ap(bk, 4), scalar1=cols[:, 8 + ft:9 + ft], scalar2=None, op0=ALU.add),
                            r=pbk(bk) + ["cols"], w=["shiftc"])
                    else:
                        P.op("dve", I("tensor_scalar",
                            out=gmod[:, ft - 8, :], in0=bank_ap(bk, 4), scalar1=b1col[:, ft - 8:ft - 7],
                            scalar2=cols[:, ft - 8:ft - 7], op0=ALU.add, op1=ALU.mult),
                            r=pbk(bk) + ["cols", "b1col"], w=["gmod"])
            else:
                g0 = c0 - 2048
                for b in range(4):
                    bk = nextbank()
                    for kt in range(8):
                        P.op("pe", I("matmul",
                            bank_ap(bk, 256), lhsT=rep[:, 4 * kt + b, :], rhs=wst[i][:, kt, :],
                            start=(kt == 0), stop=(kt == 7)),
                            r=wst_k[i] + K(REP, REP + 4096), w=pbk(bk))
                    P.op("dve", I("tensor_tensor",
                        out=gateG[:, b, g0:g0 + 256], in0=bank_ap(bk, 256), in1=bgb[:, g0:g0 + 256], op=ALU.add),
                        r=pbk(bk) + K(BGB, BGB + 1024), w=[("gateG", b)])
                    P.op("pool", I("tensor_tensor",
                        out=gateG[:, b, g0:g0 + 256], in0=gateG[:, b, g0:g0 + 256], in1=gpb[:, g0:g0 + 256], op=ALU.mult),
                        r=[("gateG", b)] + K(GPB, GPB + 1024), w=[("gateG", b)])
        for pc in range(12):
            i = stage(win_d, 256 * pc)
            copy_any(w_in_bf[:, :, 256 * pc:256 * pc + 256], wst[i], r=wst_k[i], w=[("win", pc)], engs=("dve", "pool", "act"))
        for pc in range(4):
            i = stage(wout_d, 256 * pc)
            copy_any(w_out_bf[:, :, 256 * pc:256 * pc + 256], wst[i], r=wst_k[i], w=[("wout", pc)], engs=("dve", "pool", "act"))
        i = stg[0] % 2
        stg[0] += 1
        wpv = scr[:, WST[i]:WST[i] + 512].rearrange("p (g d) -> p g d", g=4)
        P.op("sp", I("dma_start", out=wpv, in_=wpool_d.rearrange("g c d -> c g d")), w=wst_k[i], slot=("w", i))
        P.op("dve", I("tensor_copy", out=w_pool_bf[:], in_=wpv), r=wst_k[i], w=["wpool"])

        xs_rr = [0]
        zs_rr = [0]
        ts_rr = [0]
        gs_rr = [0]
        ys_rr = [0]
        pl_rr = [0]
        sg_rr = [0]
        win_all = [("win", pc) for pc in range(12)]
        wout_all = [("wout", pc) for pc in range(4)]

        jobs = [(b, c) for b in range(BPC) for c in (3, 2, 1, 0)]
        NJ = len(jobs)

        def p1a_load(n, j):
            b, c = jobs[n]
            row0 = b * S + 512 * c + 128 * j
            xs = j % 2
            P.op("sp", I("dma_start", out=xt[xs][:], in_=x_d[row0:row0 + 128, :]), w=[("xt", xs)], slot=("x", xs))

        def p1a_elem(n, j):
            xs = j % 2
            P.op("act", I("activation", out=junk, in_=xt[xs][:], func=AF.Square, accum_out=ss[:, j:j + 1]),
                 r=[("xt", xs)], w=K(JUNK, JUNK + 512) + [("ss", j)])
            P.op("pool", I("tensor_scalar", out=rs[:, j:j + 1], in0=ss[:, j:j + 1], scalar1=1.0 / D, scalar2=EPS,
                           op0=ALU.mult, op1=ALU.add), r=[("ss", j)], w=[("rs", j)])
            P.op("pool", I("tensor_tensor", out=rs[:, j:j + 1], in0=rs[:, j:j + 1], in1=neghalf[:, 0:1], op=ALU.pow),
                 r=[("rs", j), "neghalf"], w=[("rs", j)])
            P.op("dve", I("tensor_scalar", out=xt[xs][:], in0=xt[xs][:], scalar1=rs[:, j:j + 1], scalar2=None,
                          op0=ALU.mult), r=[("xt", xs), ("rs", j)], w=[("xt", xs)])

        def p1a_tr(n, j):
            b, c = jobs[n]
            xs = j % 2
            p2, jj = j // 2, j % 2
            for k in range(8):
                bk = (k // 2 + 4 * p2) % 7
                P.op("pe", I("transpose", out=bank_ap(bk, 128, (k % 2) * 256 + 128 * jj), in_=xt[xs][:, 128 * k:128 * k + 128],
                             identity=idf), r=[("xt", xs), "cst"], w=pbk(bk))
            if jj == 1:
                for k in range(8):
                    bk = (k // 2 + 4 * p2) % 7
                    src = bank_ap(bk, 256, (k % 2) * 256)
                    dst = hT[:, k, 256 * p2:256 * p2 + 256]
                    if (k // 2) % 2 == 0:
                        P.op("act", I("activation", out=dst, in_=src, func=AF.Identity,
                                      scale=gmod[:, k, b:b + 1], bias=shiftc[:, k, b:b + 1]),
                             r=pbk(bk) + ["gmod", "shiftc"], w=hT_keys(k))
                    else:
                        P.op("dve", I("tensor_scalar", out=dst, in0=src, scalar1=gmod[:, k, b:b + 1],
                                      scalar2=shiftc[:, k, b:b + 1], op0=ALU.mult, op1=ALU.add),
                             r=pbk(bk) + ["gmod", "shiftc"], w=hT_keys(k))

        def phase1_rest(n):
                b, c = jobs[n]
                tau0 = 512 * c
                hT_all = K(HT, HT + 2048)
                if DEBUG and b == 0 and c == 3:
                    P.op("sp", I("dma_start", out=dbg_d["d_hT"], in_=sview(HT, 2048, True)), r=hT_all, slot="dbg")

                def proj_fm(ft):
                    bk = nextbank()
                    for kt in range(8):
                        P.op("pe", I("matmul", bank_ap(bk), lhsT=w_in_bf[:, kt, 128 * ft:128 * ft + 128],
                                                                     rhs=hT[:, kt, :], start=(kt == 0), stop=(kt == 7)),
                             r=[("win", ft // 2)] + hT_keys(kt), w=pbk(bk))
                    return bk

                for g in range(4):
                    bk = proj_fm(g)
                    copy_any(uT[:, g, 0:512], bank_ap(bk), r=pbk(bk), w=uT_keys(g))
                for g in range(4):
                    bk = proj_fm(4 + g)
                    sgi = sg_rr[0] % 2
                    sg_rr[0] += 1
                    P.op("act", I("activation", out=sig[sgi], in_=bank_ap(bk), func=AF.Sigmoid),
                         r=pbk(bk), w=K(SIG[sgi], SIG[sgi] + 512))
                    P.op("dve", I("tensor_tensor", out=sgp[:, g, :], in0=bank_ap(bk), in1=sig[sgi], op=ALU.mult),
                         r=pbk(bk) + K(SIG[sgi], SIG[sgi] + 512), w=K(SGP + 256 * g, SGP + 256 * g + 256))
                def proj_q():
                    for p in range(4):
                        bk = proj_fm(8 + p)
                        copy_any(qT[:, p, tau0:tau0 + 512], bank_ap(bk), r=pbk(bk), w=[("qT", p, c)], scale=0.125)

                def proj_k():
                    for p in range(4):
                        bk = proj_fm(12 + p)
                        copy_any(kT[:, p, tau0:tau0 + 512], bank_ap(bk), r=pbk(bk), w=[("kT", p, c)])

                def proj_gs():
                    for p in range(4):
                        bk = proj_fm(20 + p)
                        sgi = sg_rr[0] % 2
                        sg_rr[0] += 1
                        P.op("act", I("activation", out=sig[sgi], in_=bank_ap(bk), func=AF.Sigmoid),
                             r=pbk(bk), w=K(SIG[sgi], SIG[sgi] + 512))
                        P.op("dve", I("tensor_tensor", out=sgs[:, p, :], in0=bank_ap(bk), in1=sig[sgi], op=ALU.mult),
                             r=pbk(bk) + K(SIG[sgi], SIG[sgi] + 512), w=[("sgs", p)])

                def proj_v():
                    for j in range(4):
                        bk = nextbank()
                        blk = 4 * c + j
                        for kt in range(8):
                            P.op("pe", I("matmul", bank_ap(bk), lhsT=hT[:, kt, 128 * j:128 * j + 128],
                                         rhs=w_in_bf[:, kt, 2048:2560], start=(kt == 0), stop=(kt == 7)),
                                 r=[("win", 8), ("win", 9)] + hT_keys(kt), w=pbk(bk))
                        copy_any(vv[:, blk, :], bank_ap(bk), r=pbk(bk), w=[("v", blk)])

                def pool_elem(g):
                    w = 2 << g
                    if c == 3:
                        P.op("pool", I("memset", uT[:, g, 512:528], 0.0), w=K(UT + g * 528 + 512, UT + g * 528 + 528))
                    else:
                        P.op("pool", I("tensor_copy", out=uT[:, g, 512:528], in_=halo[:, g, :]),
                             r=[("halo", g)], w=K(UT + g * 528 + 512, UT + g * 528 + 528))
                    P.op("pool", I("tensor_tensor", out=sA[:, 0:527], in0=uT[:, g, 0:527], in1=uT[:, g, 1:528], op=ALU.add),
                         r=uT_keys(g), w=K(SA, SA + 528))
                    cur, curk, oth, othk = sA, K(SA, SA + 528), sB, K(SB, SB + 528)
                    ln = 527
                    step = 2
                    while step < w:
                        nl = ln - step
                        P.op("pool", I("tensor_tensor",
                            out=oth[:, 0:nl], in0=cur[:, 0:nl], in1=cur[:, step:step + nl], op=ALU.add), r=curk, w=othk)
                        cur, curk, oth, othk = oth, othk, cur, curk
                        ln = nl
                        step *= 2
                    pli = pl_rr[0] % 2
                    pl_rr[0] += 1
                    plk = K(PL[pli], PL[pli] + 256)
                    P.op("dve", I("scalar_tensor_tensor",
                        out=pl[pli], in0=cur[:, 0:512], scalar=1.0 / w, in1=uT[:, g, 0:512], op0=ALU.mult, op1=ALU.subtract),
                        r=curk + uT_keys(g), w=plk)
                    if c == 3:
                        lo = 513 - w
                        P.op("dve", I("tensor_tensor",
                            out=etmp[:, 0:w - 1], in0=cur[:, lo:512], in1=cnt[:, 16 * g:16 * g + w - 1], op=ALU.mult),
                            r=curk + ["cst"], w=["etmp"])
                        P.op("dve", I("tensor_tensor",
                            out=pl[pli][:, lo:512], in0=etmp[:, 0:w - 1], in1=uT[:, g, lo:512], op=ALU.subtract),
                            r=["etmp"] + uT_keys(g), w=plk)
                    P.op("pool", I("tensor_copy", out=halo[:, g, :], in_=uT[:, g, 0:16]), r=uT_keys(g), w=[("halo", g)])
                    return pli, plk

                def pool_mm(g, pli, plk):
                    bk = nextbank()
                    P.op("pe", I("matmul", bank_ap(bk), lhsT=w_pool_bf[:, g, :], rhs=pl[pli], start=True, stop=True),
                         r=["wpool"] + plk, w=pbk(bk))
                    P.op("dve", I("scalar_tensor_tensor",
                        out=ycat[:, g, :], in0=bank_ap(bk), scalar=cols[:, 24 + g:25 + g], in1=sgp[:, g, :], op0=ALU.mult, op1=ALU.mult),
                        r=pbk(bk) + ["cols"] + K(SGP + 256 * g, SGP + 256 * g + 256), w=[("yc", g)])

                pe0 = pool_elem(0)
                proj_q()
                pool_mm(0, *pe0)
                pe1 = pool_elem(1)
                proj_k()
                pool_mm(1, *pe1)
                pe2 = pool_elem(2)
                proj_gs()
                pool_mm(2, *pe2)
                pe3 = pool_elem(3)
                proj_v()
                pool_mm(3, *pe3)

                if DEBUG and b == 0 and c == 3:
                    P.op("sp", I("dma_start", out=dbg_d["d_uT"], in_=sview(UT, 2112)), r=K(UT, UT + 2112), slot="dbg")
                    P.op("sp", I("dma_start", out=dbg_d["d_sgp"], in_=sview(SGP, 1024, True)), r=K(SGP, SGP + 1024), slot="dbg")
                    P.op("sp", I("dma_start", out=dbg_d["d_sgs"], in_=sgs[:].rearrange("p g t -> p (g t)")), r=[("sgs", p) for p in range(4)], slot="dbg")
                if DEBUG and b == 0 and c == 0:
                    P.op("sp", I("dma_start", out=dbg_d["d_qT"], in_=qT[:].rearrange("p g t -> p (g t)")), r=[("qT", p, cc) for p in range(4) for cc in range(4)], slot="dbg")
                    P.op("sp", I("dma_start", out=dbg_d["d_kT"], in_=kT[:].rearrange("p g t -> p (g t)")), r=[("kT", p, cc) for p in range(4) for cc in range(4)], slot="dbg")
                    P.op("sp", I("dma_start", out=dbg_d["d_vv"], in_=vv[:].rearrange("p g t -> p (g t)")), r=[("v", kb) for kb in range(16)], slot="dbg")
                    P.op("sp", I("dma_start", out=dbg_d["d_gmod"], in_=gmod[:].rearrange("p g t -> p (g t)")), r=["gmod"], slot="dbg")
                    P.op("sp", I("dma_start", out=dbg_d["d_shiftc"], in_=shiftc[:].rearrange("p g t -> p (g t)")), r=["shiftc"], slot="dbg")
                    P.op("sp", I("dma_start", out=dbg_d["d_gateG"], in_=gateG[:].rearrange("p g t -> p (g t)")), r=[("gateG", bb) for bb in range(4)], slot="dbg")
        def attention(n):
                b, c = jobs[n]
                tau0 = 512 * c
                segs = []
                for p in range(4):
                    for e2 in range(2):
                        hs = []
                        for j in range(4):
                            tq = 128 * (4 * c + j)
                            k0 = tq
                            si = 0
                            while k0 < S:
                                wseg = min(1024, S - k0)
                                hs.append([p, e2, j, tq, k0, wseg, si, False])
                                k0 += wseg
                                si += 1
                        hs[-1][7] = True
                        segs.extend(hs)
                pend = []
                prev_ps = None
                prev_w = None
                hcount = [0]

                def stage2(item):
                    (p, e2, j, k0, wseg, gsl, last) = item
                    lo_p, hi_p = 64 * e2, 64 * e2 + 64
                    ts = ts_rr[0] % 2
                    ts_rr[0] += 1
                    tb = ps[:, 2048 + 512 * ts: 2048 + 512 * ts + 512].bitcast(BF16)
                    ak = K(AB[gsl], AB[gsl] + 512)
                    for kk in range(wseg // 128):
                        P.op("pe", I("transpose",
                            out=tb[:, 128 * kk:128 * kk + 128], in_=abuf[gsl][:, 128 * kk:128 * kk + 128], identity=idb[:]),
                            r=ak + ["idb"], w=pbk(4 + ts))
                    nb = wseg // 128
                    kb0 = k0 // 128
                    P.op("act", I("activation",
                        out=AT[:, kb0:kb0 + nb, 128 * j:128 * j + 128],
                        in_=tb[:, 0:wseg].rearrange("p (k t) -> p k t", t=128), func=AF.Copy),
                        r=pbk(4 + ts), w=AT_keys(kb0, nb, j))
                    if last:
                        zs_ = zs_rr[0] % 2
                        zs_rr[0] += 1
                        obk = 2 * zs_
                        ocol = 512 * obk
                        for kb in range(15, 4 * c - 1, -1):
                            i = kb - 4 * c
                            N = 512 if i >= 3 else 128 * (i + 1)
                            P.op("pe", I("matmul",
                                ps[:, ocol:ocol + N], lhsT=vv[:, kb, 128 * p:128 * p + 128], rhs=AT[:, kb, 0:N],
                                start=(kb == 15), stop=(kb == 4 * c)),
                                r=[("v", kb)] + AT_keys_n(kb, N), w=pbk(obk))
                        P.op("dve", I("tensor_tensor",
                            out=ycat[lo_p:hi_p, 4 + p, :], in0=ps[lo_p:hi_p, ocol:ocol + 512], in1=sgs[lo_p:hi_p, p, :], op=ALU.mult),
                            r=pbk(obk) + [("sgs", p)], w=[("yc", 4 + p, e2)])

                for (p, e2, j, tq, k0, wseg, si, last) in segs:
                    lo_p, hi_p = 64 * e2, 64 * e2 + 64
                    zs = zs_rr[0] % 2
                    zs_rr[0] += 1
                    gsl = gs_rr[0] % NG
                    psl = gs_rr[0] % NPB
                    asl = gs_rr[0] % NA
                    gs_rr[0] += 1
                    zcol = 1024 * zs
                    zk = pbk(2 * zs, 2 * zs + 1)
                    npc = (wseg + 511) // 512
                    for i in range(npc):
                        wi = min(512, wseg - 512 * i)
                        first = (si == 0 and i == 0)
                        kchunks = sorted(set([(k0 + 512 * i) // 512, (k0 + 512 * i + wi - 1) // 512]))
                        P.op("pe", I("matmul",
                            ps[:, zcol + 512 * i: zcol + 512 * i + wi], lhsT=qT[lo_p:hi_p, p, tq:tq + 128],
                            rhs=kT[lo_p:hi_p, p, k0 + 512 * i: k0 + 512 * i + wi], start=True, stop=(not first)),
                            r=[("qT", p, c)] + [("kT", p, kc) for kc in kchunks], w=pbk(2 * zs + i))
                        if first:
                            P.op("pe", I("matmul", ps[:, zcol:zcol + 128], lhsT=idb[:], rhs=mkb[:], start=False, stop=True),
                                 r=["idb", "mkb"], w=pbk(2 * zs))
                    nd_ = 512
                    for _d in range(2):
                        P.op("pe", I("matmul", ps[:, 3072:3072 + nd_], lhsT=idb[:], rhs=qT[:, p, tau0:tau0 + nd_], start=True, stop=True),
                             r=["idb", ("qT", p, c)], w=pbk(6))
                    gk = K(GB[gsl], GB[gsl] + 1024)
                    pk = K(PB[psl], PB[psl] + 1088)
                    ak = K(AB[asl], AB[asl] + 512)
                    P.op("act", I("activation",
                        out=gbuf[gsl][:, 0:wseg], in_=ps[:, zcol:zcol + wseg], func=AF.Sigmoid, scale=-1.0),
                        r=zk[:npc], w=gk)
                    if si == 0:
                        P.op("dve", I("tensor_tensor_scan",
                            out=pbuf[psl][:, 1:1 + wseg], data0=gbuf[gsl][:, 0:wseg], data1=ps[:, 3584:3585].to_broadcast([128, wseg]),
                            initial=1.0, op0=ALU.mult, op1=ALU.mult), r=gk + [("pb", 7)], w=pk)
                        P.op("pool", I("tensor_scalar", out=abuf[asl][:, 0:1], in0=pbuf[psl][:, 1:2], scalar1=-1.0, scalar2=1.0,
                                       op0=ALU.mult, op1=ALU.add), r=pk, w=ak)
                    else:
                        ppk = K(PB[prev_ps], PB[prev_ps] + 1088)
                        P.op("dve", I("tensor_tensor_scan",
                            out=pbuf[psl][:, 1:1 + wseg], data0=gbuf[gsl][:, 0:wseg], data1=ps[:, 3584:3585].to_broadcast([128, wseg]),
                            initial=pbuf[prev_ps][:, prev_w:prev_w + 1], op0=ALU.mult, op1=ALU.mult), r=gk + ppk + [("pb", 7)], w=pk)
                        P.op("pool", I("tensor_tensor", out=abuf[asl][:, 0:1], in0=pbuf[prev_ps][:, prev_w:prev_w + 1],
                                       in1=pbuf[psl][:, 1:2], op=ALU.subtract), r=pk + ppk, w=ak)
                    P.op("pool", I("tensor_tensor",
                        out=abuf[asl][:, 1:wseg], in0=pbuf[psl][:, 1:wseg], in1=pbuf[psl][:, 2:1 + wseg], op=ALU.subtract),
                        r=pk, w=ak)
                    prev_ps, prev_w = psl, wseg
                    pend.append((p, e2, j, k0, wseg, asl, last))
                    if len(pend) > LAG:
                        stage2(pend.pop(0))
                while pend:
                    stage2(pend.pop(0))

        def p3_reload(n, j):
            b, c = jobs[n]
            row0 = b * S + 512 * c + 128 * j
            xs = 2 + j % 2
            P.op("sp", I("dma_start", out=xt[xs][:], in_=x_d[row0:row0 + 128, :]), w=[("xt", xs)], slot=("x", xs))

        def phase3(n, js=(0, 1, 2, 3)):
                b, c = jobs[n]
                tau0 = 512 * c
                yc_all = [("yc", g) for g in range(4)] + [("yc", 4 + p, e2) for p in range(4) for e2 in range(2)]
                if DEBUG and b == 0 and c == 3:
                    P.op("sp", I("dma_start", out=dbg_d["d_ycat"], in_=ycat[:].rearrange("p g t -> p (g t)")), r=yc_all, slot="dbg")
                for j in js:
                    row0 = b * S + tau0 + 128 * j
                    zs = j % 3
                    zcol = 1024 * zs
                    for half in range(2):
                        for ft in range(8):
                            P.op("pe", I("matmul",
                                ps[:, zcol + 512 * half: zcol + 512 * half + 512], lhsT=ycat[:, ft, 128 * j:128 * j + 128],
                                rhs=w_out_bf[:, ft, 512 * half:512 * half + 512], start=(ft == 0), stop=(ft == 7)),
                                r=yc_all + wout_all, w=pbk(2 * zs + half))
                    yk = pbk(2 * zs, 2 * zs + 1)
                    P.op("act", I("activation", out=junk2, in_=ps[:, zcol:zcol + 1024], func=AF.Square,
                                                                        accum_out=ss2[:, j:j + 1]),
                         r=yk, w=K(JUNK2, JUNK2 + 512) + [("ss2", j)])
                    P.op("pool", I("tensor_scalar", out=rs2[:, j:j + 1], in0=ss2[:, j:j + 1], scalar1=1.0 / D, scalar2=EPS,
                                                                 op0=ALU.mult, op1=ALU.add), r=[("ss2", j)], w=[("rs2", j)])
                    P.op("pool", I("tensor_tensor", out=rs2[:, j:j + 1], in0=rs2[:, j:j + 1], in1=neghalf[:, 0:1], op=ALU.pow),
                         r=[("rs2", j), "neghalf"], w=[("rs2", j)])
                    xs = 2 + j % 2
                    ysi = ys_rr[0] % 2
                    ys_rr[0] += 1
                    ysk = K(YS[ysi], YS[ysi] + 1024)
                    P.op("dve", I("scalar_tensor_tensor",
                        out=ysb[ysi], in0=ps[:, zcol:zcol + 1024], scalar=rs2[:, j:j + 1], in1=gateG[:, b, :],
                        op0=ALU.mult, op1=ALU.mult), r=yk + [("rs2", j), ("gateG", b)], w=ysk)
                    P.op("pool", I("tensor_tensor", out=ysb[ysi], in0=ysb[ysi], in1=xt[xs][:], op=ALU.add),
                         r=ysk + [("xt", xs)], w=ysk)
                    if j + 2 < 4:
                        p3_reload(n, j + 2)
                    P.op("sp", I("dma_start", out=out_d[row0:row0 + 128, :], in_=ysb[ysi]),
                         r=ysk, slot=("o", ysi))
        def phase1_second_half(n, with_elem=True):
            if with_elem:
                p1a_elem(n, 2)
                p1a_elem(n, 3)
            p1a_tr(n, 2)
            p1a_tr(n, 3)
            phase1_rest(n)

        def phase1_tr01(n):
            p1a_tr(n, 0)
            p1a_tr(n, 1)
            p1a_load(n, 2)
            p1a_load(n, 3)

        p1a_load(0, 0)
        p1a_load(0, 1)
        p1a_elem(0, 0)
        p1a_elem(0, 1)
        phase1_tr01(0)
        phase1_second_half(0)
        for n in range(NJ):
            p3_reload(n, 0)
            p3_reload(n, 1)
            if n + 1 < NJ:
                p1a_load(n + 1, 0)
                p1a_load(n + 1, 1)
            attention(n)
            if n + 1 < NJ:
                p1a_elem(n + 1, 0)
                p1a_elem(n + 1, 1)
                phase1_tr01(n + 1)
            phase3(n, (0, 1))
            if n + 1 < NJ:
                p1a_elem(n + 1, 2)
                p1a_elem(n + 1, 3)
            phase3(n, (2, 3))
            if n + 1 < NJ:
                phase1_second_half(n + 1, with_elem=False)
        P.emit(nc)
    return nc


_NC_CACHE = {}


def kernel(x, c, w_ada, b_ada, g_pre, w_in, w_pool, pool_scale, w_out, g_post):
    f32 = np.float32
    x = np.asarray(x, f32)
    c = np.asarray(c, f32)
    w_ada = np.ascontiguousarray(np.asarray(w_ada, f32))
    b_ada = np.asarray(b_ada, f32)
    g_pre = np.asarray(g_pre, f32)
    w_in = np.ascontiguousarray(np.asarray(w_in, f32))
    w_pool = np.ascontiguousarray(np.asarray(w_pool, f32))
    pool_scale = np.asarray(pool_scale, f32)
    w_out = np.ascontiguousarray(np.asarray(w_out, f32))
    g_post = np.asarray(g_post, f32)

    ident = np.eye(128, dtype=f32)
    ii = np.arange(128)
    maskb = np.where(ii[None, :] <= ii[:, None], -30000.0, 0.0).astype(f32)
    cnt = np.ones((4, 16), f32)
    for g in range(4):
        w = 2 << g
        for i in range(w - 1):
            cnt[g, i] = 1.0 / (w - 1 - i)
    cst = np.concatenate([ident, maskb, np.broadcast_to(cnt.reshape(1, 64), (128, 64))], axis=1).astype(f32)
    cols = np.concatenate([g_pre.reshape(8, 128).T, b_ada[:2048].reshape(16, 128).T, pool_scale.reshape(4, 128).T], axis=1)
    cols = np.ascontiguousarray(cols, f32)
    rows = np.ascontiguousarray(np.stack([b_ada[2048:3072], g_post], axis=0), f32)

    in_maps = []
    for i in range(NCORES):
        xc = np.ascontiguousarray(x[BPC * i:BPC * (i + 1), ::-1, :]).reshape(BPC * S, D)
        cc = c[BPC * i:BPC * (i + 1)]
        ct = np.ascontiguousarray(cc.T.reshape(8, 128, BPC).transpose(1, 0, 2).reshape(128, 32), f32)
        in_maps.append({"x": xc, "ct": ct, "cols": cols, "rows": rows, "cst": cst, "w_ada": w_ada,
                        "w_in": w_in, "w_pool": w_pool, "w_out": w_out})
    if "nc" not in _NC_CACHE:
        _NC_CACHE["nc"] = build_nc()
    res = run_bass_kernel_spmd(_NC_CACHE["nc"], in_maps, core_ids=list(range(NCORES)))
    _NC_CACHE["res"] = res
    out = np.empty((NCORES * BPC, S, D), f32)
    for i in range(NCORES):
        o = np.asarray(res.results[i]["out"], f32).reshape(BPC, S, D)
        out[BPC * i:BPC * (i + 1)] = o[:, ::-1, :]
    return out
```
